# Optimizing a Trainium2 kernel written in Bass

```python
import math
import jax, jax.numpy as jnp
from jax import lax
import numpy as np

D_MODEL = 1024
BATCH = 8
SEQ = 8192
DEPTH = 1
DEC_BATCH = 32
DEC_SEQ = 16
PAST_LEN = 1024

CHUNK = 64
N_HEADS = 16
HEAD_DIM = 64
KV_HEADS = 4
GROUP = N_HEADS // KV_HEADS
IDX_HEADS = 8
IDX_DIM = 64
TOPK_MAX = 256
Q_BLOCK = 128
GM_CHUNK = 128
GM_GROUPS = 4
GM_WIDTH = D_MODEL
GM_GROUP_DIM = GM_WIDTH // GM_GROUPS
D_FF = 2816
REL_BUCKETS = 32
REL_MAX_DIST = 128
LN_EPS = 1e-5
ALPHA = (2 * DEPTH) ** 0.25
BETA = (8 * DEPTH) ** -0.25
ATT_Q = N_HEADS * HEAD_DIM
ATT_KV = KV_HEADS * HEAD_DIM
IDX_Q = IDX_HEADS * IDX_DIM
IN_SPLITS = (ATT_Q, ATT_KV, ATT_KV, IDX_Q, IDX_DIM, IDX_HEADS, 2 * GM_WIDTH, D_MODEL, D_MODEL)
IN_WIDTH = ATT_Q + 2 * ATT_KV + IDX_Q + IDX_DIM + IDX_HEADS + 2 * GM_WIDTH + 2 * D_MODEL

kernel_name = "dsa_gmlp_gated_streaming_encoder_step"


def layer_norm(x, g, b):
    xf = x.astype(jnp.float32)
    mu = jnp.mean(xf, axis=-1, keepdims=True)
    var = jnp.mean(jnp.square(xf - mu), axis=-1, keepdims=True)
    return ((xf - mu) * lax.rsqrt(var + LN_EPS)).astype(x.dtype) * g + b


def swiglu(x, wg, wu, wd):
    return (jax.nn.silu(x @ wg) * (x @ wu)) @ wd


def rel_bucket(rel):
    half = REL_BUCKETS // 2
    max_exact = half // 2
    ret = jnp.where(rel > 0, half, 0)
    n = jnp.abs(rel)
    nf = jnp.maximum(n, 1).astype(jnp.float32)
    large = max_exact + (jnp.log(nf / max_exact) / math.log(REL_MAX_DIST / max_exact)
                         * (half - max_exact)).astype(jnp.int32)
    large = jnp.minimum(large, half - 1)
    return ret + jnp.where(n < max_exact, n, large)


def dsa_attend(q, qi, wi, q_pos, n_vis, k, v, ki, rel_table, n_sel):
    f32 = jnp.float32
    B, Q = q.shape[:2]
    L = k.shape[1]
    dots = jnp.einsum('bqhd,bsd->bqhs', qi, ki).astype(f32) * IDX_DIM ** -0.5
    score = jnp.einsum('bqhs,bqh->bqs', jax.nn.relu(dots), wi.astype(f32)) * IDX_HEADS ** -0.5
    key_ok = jnp.arange(L, dtype=jnp.int32)[None, :] < n_vis[:, None]
    score = jnp.where(key_ok[None], score, -jnp.inf)
    _, sel = lax.top_k(score, n_sel)
    valid = sel < n_vis[None, :, None]
    gather = jax.vmap(lambda rows, ids: rows[ids])
    kg = gather(k, sel)
    vg = gather(v, sel)
    qg = q.reshape(B, Q, KV_HEADS, GROUP, HEAD_DIM)
    logits = jnp.einsum('bqkgd,bqnkd->bqkgn', qg, kg).astype(f32) * HEAD_DIM ** -0.5
    bias = rel_table[rel_bucket(sel - q_pos[None, :, None])].astype(f32)
    bias = bias.reshape(B, Q, n_sel, KV_HEADS, GROUP).transpose(0, 1, 3, 4, 2)
    logits = jnp.where(valid[:, :, None, None, :], logits + bias, -1e30)
    p = jax.nn.softmax(logits, axis=-1).astype(v.dtype)
    out = jnp.einsum('bqkgn,bqnkd->bqkgd', p, vg)
    return out.reshape(B, Q, N_HEADS * HEAD_DIM)


def dsa_prompt(q, qi, wi, k, v, ki, rel_table):
    B, S = q.shape[:2]
    nb = S // Q_BLOCK
    n_sel = min(TOPK_MAX, S // 4)
    pos = jnp.arange(S, dtype=jnp.int32)
    n_vis = (pos // CHUNK + 1) * CHUNK

    def blk(a):
        return jnp.swapaxes(a.reshape(B, nb, Q_BLOCK, *a.shape[2:]), 0, 1)

    def one(args):
        qb, qib, wib, pb, vb = args
        return dsa_attend(qb, qib, wib, pb, vb, k, v, ki, rel_table, n_sel)

    out = lax.map(one, (blk(q), blk(qi), blk(wi), pos.reshape(nb, Q_BLOCK), n_vis.reshape(nb, Q_BLOCK)))
    return jnp.swapaxes(out, 0, 1).reshape(B, S, N_HEADS * HEAD_DIM)


def spatial_gate(u, v, w_s, b_s):
    B, T = u.shape[:2]
    rows = min(T, GM_CHUNK)
    n = T // rows
    mask = jnp.tril(jnp.ones((GM_CHUNK, GM_CHUNK), w_s.dtype))
    ws = (w_s * mask)[:, :rows, :rows]
    vr = v.reshape(B, n, rows, GM_GROUPS, GM_GROUP_DIM)
    s = jnp.einsum('gij,bnjgc->bnigc', ws, vr) + b_s[:, :rows].T[None, None, :, :, None]
    return u * s.reshape(B, T, GM_WIDTH)


def mixer_inputs(h, w_in, gm_ln_g, gm_ln_b):
    z = h @ w_in
    parts = []
    off = 0
    for w in IN_SPLITS:
        parts.append(z[..., off:off + w])
        off += w
    q, k, v, qi, ki, wi, zg, ga, gb = parts
    B, T = h.shape[:2]
    q = q.reshape(B, T, N_HEADS, HEAD_DIM)
    k = k.reshape(B, T, KV_HEADS, HEAD_DIM)
    v = v.reshape(B, T, KV_HEADS, HEAD_DIM)
    qi = qi.reshape(B, T, IDX_HEADS, IDX_DIM)
    zg = jax.nn.gelu(zg)
    u = zg[..., :GM_WIDTH]
    vg = layer_norm(zg[..., GM_WIDTH:], gm_ln_g, gm_ln_b)
    return q, k, v, qi, ki, wi, u, vg, ga, gb


def run_layer(x, mixer, ln1_g, ln1_b, f1g, f1u, f1d, w_in, gm_ln_g, gm_ln_b, gm_ws, gm_bs,
              w_br_a, w_br_b, w_out, ln2_g, ln2_b, f2g, f2u, f2d, ln3_g, ln3_b):
    h = layer_norm(ALPHA * x + 0.5 * swiglu(x, f1g, f1u, f1d), ln1_g, ln1_b)
    q, k, v, qi, ki, wi, u, vg, ga, gb = mixer_inputs(h, w_in, gm_ln_g, gm_ln_b)
    a, rows = mixer(q, k, v, qi, ki, wi)
    g = spatial_gate(u, vg, gm_ws, gm_bs)
    mixed = (jax.nn.sigmoid(ga) * (a @ w_br_a) + jax.nn.sigmoid(gb) * (g @ w_br_b)) @ w_out
    h = layer_norm(ALPHA * h + mixed, ln2_g, ln2_b)
    y = layer_norm(ALPHA * h + 0.5 * swiglu(h, f2g, f2u, f2d), ln3_g, ln3_b)
    return y, rows, vg


def setup_inputs(seed: int = 0) -> dict:
    key = jax.random.key(seed)
    ks = iter(jax.random.split(key, 32))

    def nrm(shape, scale):
        return jax.random.normal(next(ks), shape, jnp.float32) * scale

    col_scale = jnp.ones((IN_WIDTH,), jnp.float32).at[ATT_Q + ATT_KV:ATT_Q + 2 * ATT_KV].set(BETA)
    return {
        "x_prompt": nrm((BATCH, SEQ, D_MODEL), 1.0),
        "x_sample": nrm((DEC_BATCH, DEC_SEQ, D_MODEL), 1.0),
        "cache_k": nrm((DEPTH, DEC_BATCH, PAST_LEN, KV_HEADS, HEAD_DIM), 1.0),
        "cache_v": nrm((DEPTH, DEC_BATCH, PAST_LEN, KV_HEADS, HEAD_DIM), BETA),
        "cache_kidx": nrm((DEPTH, DEC_BATCH, PAST_LEN, IDX_DIM), 1.0),
        "rel_table": nrm((REL_BUCKETS, N_HEADS), 0.5),
        "ln1_g": 1.0 + nrm((DEPTH, D_MODEL), 0.02),
        "ln1_b": nrm((DEPTH, D_MODEL), 0.02),
        "ffn1_wg": nrm((DEPTH, D_MODEL, D_FF), D_MODEL ** -0.5),
        "ffn1_wu": nrm((DEPTH, D_MODEL, D_FF), D_MODEL ** -0.5),
        "ffn1_wd": nrm((DEPTH, D_FF, D_MODEL), D_FF ** -0.5 * BETA),
        "w_in": nrm((DEPTH, D_MODEL, IN_WIDTH), D_MODEL ** -0.5) * col_scale,
        "gm_ln_g": 1.0 + nrm((DEPTH, GM_WIDTH), 0.02),
        "gm_ln_b": nrm((DEPTH, GM_WIDTH), 0.02),
        "gm_ws": nrm((DEPTH, GM_GROUPS, GM_CHUNK, GM_CHUNK), GM_CHUNK ** -0.5),
        "gm_bs": 1.0 + nrm((DEPTH, GM_GROUPS, GM_CHUNK), 0.02),
        "w_br_a": nrm((DEPTH, ATT_Q, D_MODEL), ATT_Q ** -0.5 * BETA),
        "w_br_b": nrm((DEPTH, GM_WIDTH, D_MODEL), GM_WIDTH ** -0.5 * BETA),
        "w_out": nrm((DEPTH, D_MODEL, D_MODEL), D_MODEL ** -0.5 * BETA),
        "ln2_g": 1.0 + nrm((DEPTH, D_MODEL), 0.02),
        "ln2_b": nrm((DEPTH, D_MODEL), 0.02),
        "ffn2_wg": nrm((DEPTH, D_MODEL, D_FF), D_MODEL ** -0.5),
        "ffn2_wu": nrm((DEPTH, D_MODEL, D_FF), D_MODEL ** -0.5),
        "ffn2_wd": nrm((DEPTH, D_FF, D_MODEL), D_FF ** -0.5 * BETA),
        "ln3_g": 1.0 + nrm((DEPTH, D_MODEL), 0.02),
        "ln3_b": nrm((DEPTH, D_MODEL), 0.02),
    }


def reference(x_prompt, x_sample, cache_k, cache_v, cache_kidx, rel_table,
              ln1_g, ln1_b, ffn1_wg, ffn1_wu, ffn1_wd, w_in, gm_ln_g, gm_ln_b, gm_ws, gm_bs,
              w_br_a, w_br_b, w_out, ln2_g, ln2_b, ffn2_wg, ffn2_wu, ffn2_wd, ln3_g, ln3_b):
    past = cache_k.shape[2]
    n_new = x_sample.shape[1]
    total = past + n_new
    n_sel_sample = min(TOPK_MAX, total // 4)
    pos_s = past + jnp.arange(n_new, dtype=jnp.int32)
    vis_s = jnp.full((n_new,), total, jnp.int32)

    xp, xs = x_prompt, x_sample
    kp_l, vp_l, kip_l, ks_l, vs_l, kis_l, gv_l = [], [], [], [], [], [], []
    for l in range(DEPTH):
        lw = (ln1_g[l], ln1_b[l], ffn1_wg[l], ffn1_wu[l], ffn1_wd[l], w_in[l], gm_ln_g[l], gm_ln_b[l],
              gm_ws[l], gm_bs[l], w_br_a[l], w_br_b[l], w_out[l], ln2_g[l], ln2_b[l],
              ffn2_wg[l], ffn2_wu[l], ffn2_wd[l], ln3_g[l], ln3_b[l])

        def prompt_mixer(q, k, v, qi, ki, wi):
            return dsa_prompt(q, qi, wi, k, v, ki, rel_table), (k, v, ki)

        def sample_mixer(q, k, v, qi, ki, wi, l=l):
            k_all = jnp.concatenate([cache_k[l].astype(k.dtype), k], axis=1)
            v_all = jnp.concatenate([cache_v[l].astype(v.dtype), v], axis=1)
            ki_all = jnp.concatenate([cache_kidx[l].astype(ki.dtype), ki], axis=1)
            a = dsa_attend(q, qi, wi, pos_s, vis_s, k_all, v_all, ki_all, rel_table, n_sel_sample)
            return a, (k, v, ki)

        xp, (kp, vp, kip), _ = run_layer(xp, prompt_mixer, *lw)
        xs, (ks_, vs_, kis), gv = run_layer(xs, sample_mixer, *lw)
        kp_l.append(kp); vp_l.append(vp); kip_l.append(kip)
        ks_l.append(ks_); vs_l.append(vs_); kis_l.append(kis); gv_l.append(gv)

    return (xp, xs, jnp.stack(kp_l), jnp.stack(vp_l), jnp.stack(kip_l),
            jnp.stack(ks_l), jnp.stack(vs_l), jnp.stack(kis_l), jnp.stack(gv_l))
```

```python
from contextlib import ExitStack
import os
import numpy as np
import concourse.bass as bass
DBG = os.environ.get('KDBG', '')
KATT = int(os.environ.get('KATT', '99'))
import concourse.mybir as mybir
from concourse.bass_utils import run_bass_kernel_spmd

F32 = mybir.dt.float32
BF16 = mybir.dt.bfloat16
U8 = mybir.dt.uint8
AF = mybir.ActivationFunctionType
ALU = mybir.AluOpType
AX = mybir.AxisListType

D = 1024
DFF = 2816
INW = 6216
NEG = -1.0e30
ALPHA = 2.0 ** 0.25
NIT = 24
TT = 256
PAST = 1024
NS = 16
NSTREAM = 4
LX = 1152


class Buf:
    __slots__ = ("name", "w", "r", "excl")

    def __init__(self, name="", excl=False):
        self.name = name
        self.w = None
        self.r = {}
        self.excl = excl


class FW:
    def __init__(self, nc, es):
        self.nc = nc
        self.es = es
        self.eng = {"pe": nc.tensor, "act": nc.scalar, "dve": nc.vector, "pool": nc.gpsimd, "sp": nc.sync}
        self.prog = {e: [] for e in self.eng}
        self.sems = {}
        self.cnt = {}
        self.waited = {e: {} for e in self.eng}
        self.dq = {}
        self.KDMA = 8
        self.n = 0

    def _sem(self, key):
        if key not in self.sems:
            self.sems[key] = self.es.enter_context(self.nc.semaphore("s_" + "_".join(str(k) for k in key)))
            self.cnt[key] = 0
        return self.sems[key]

    def _need(self, e, waits, ev, same_ok):
        if ev is None:
            return
        key, val = ev
        if key == (e,) and same_ok and e == "pe":
            return
        if self.waited[e].get(key, 0) >= val:
            return
        waits[key] = max(waits.get(key, 0), val)

    def _deps(self, e, reads, writes):
        waits = {}
        for b in reads:
            self._need(e, waits, b.w, False)
            if b.excl:
                for k, v in b.r.items():
                    if k != (e,):
                        self._need(e, waits, (k, v), True)
        for b in writes:
            self._need(e, waits, b.w, True)
            for k, v in b.r.items():
                self._need(e, waits, (k, v), True)
        for k, v in waits.items():
            self.waited[e][k] = v
        return [(self._sem(k), v) for k, v in waits.items()]

    def op(self, e, fn, reads=(), writes=(), inc=True):
        self.n += 1
        waits = self._deps(e, reads, writes)
        key = (e,)
        sem = self._sem(key)
        if inc:
            self.cnt[key] += 1
            val = self.cnt[key]
        else:
            val = self.cnt[key] + 1
        for b in reads:
            if b.r.get(key, 0) < val:
                b.r[key] = val
        for b in writes:
            b.w = (key, val)
            b.r = {}
        self.prog[e].append((waits, fn, (sem, 1) if inc else None))

    def dma(self, q, out, in_, reads=(), writes=(), slow=False):
        self.n += 1
        i = self.dq.get(q, 0)
        self.dq[q] = i + 1
        key = ("dma", q, i % self.KDMA)
        sem = self._sem(key)
        waits = self._deps(q, reads, writes)
        prev = self.cnt[key]
        if prev > 0 and self.waited[q].get(key, 0) < prev:
            waits.append((sem, prev))
            self.waited[q][key] = prev
        self.cnt[key] += 16
        val = self.cnt[key]
        for b in reads:
            b.r[key] = val
        for b in writes:
            b.w = (key, val)
            b.r = {}
        if slow:
            self.prog[q].append((waits, lambda eng: eng.dma_start(out=out, in_=in_, allow_slow_non_contiguous=True), (sem, 16)))
        else:
            self.prog[q].append((waits, lambda eng: eng.dma_start(out=out, in_=in_), (sem, 16)))

    def barrier(self):
        for e in self.eng:
            waits = []
            for k, c in self.cnt.items():
                if c > 0 and k != (e,) and self.waited[e].get(k, 0) < c:
                    waits.append((self._sem(k), c))
                    self.waited[e][k] = c
            if waits:
                self.prog[e].append((waits, None, None))

    def finish(self):
        waits = [(self._sem(k), c) for k, c in self.cnt.items() if c > 0]
        self.prog["sp"].append((waits, None, None))

    def emit(self):
        progs = self.prog
        self.prog = {e: [] for e in self.eng}
        for e, lst in progs.items():
            eng = self.eng[e]
            for waits, fn, inc in lst:
                for sem, v in waits:
                    eng.wait_ge(sem, v)
                if fn is not None:
                    ins = fn(eng)
                    if inc is not None:
                        ins.then_inc(inc[0], inc[1])


def hidx_head(hidx):
    return hidx


def chunk_heads(c):
    p, i = divmod(c, 4)
    return 8 * p + i, 8 * p + 4 + i


def rel_bucket_jx(rel):
    import math
    import jax
    import jax.numpy as jnp
    with jax.default_device(jax.devices("cpu")[0]):
        rel = jnp.asarray(rel, jnp.int32)
        half, max_exact = 16, 8
        ret = jnp.where(rel > 0, half, 0)
        n = jnp.abs(rel)
        nf = jnp.maximum(n, 1).astype(jnp.float32)
        large = max_exact + (jnp.log(nf / max_exact) / math.log(128 / max_exact) * (half - max_exact)).astype(jnp.int32)
        large = jnp.minimum(large, half - 1)
        return np.asarray(ret + jnp.where(n < max_exact, n, large))


def rel_bucket_np(rel):
    half, max_exact = 16, 8
    ret = np.where(rel > 0, half, 0)
    n = np.abs(rel)
    nf = np.maximum(n, 1).astype(np.float32)
    large = max_exact + (np.log(nf / np.float32(max_exact)) / np.float32(np.log(128.0 / max_exact))
                         * np.float32(half - max_exact)).astype(np.int32)
    large = np.minimum(large, half - 1)
    return ret + np.where(n < max_exact, n, large)


def host_consts():
    c = {}
    c["c_ident"] = np.eye(128, dtype=np.float32)
    c["c_exch"] = np.ascontiguousarray(np.eye(128, dtype=np.float32)[::-1])
    c["c_triu"] = np.triu(np.ones((128, 128), np.float32))
    rel = np.arange(384, dtype=np.int64) - 255
    try:
        b = rel_bucket_jx(rel.astype(np.int32))
    except Exception:
        b = rel_bucket_np(rel.astype(np.int32))
    oh = np.zeros((32, 384), np.float32)
    oh[b, np.arange(384)] = 1.0
    c["c_onehot"] = oh
    k = np.arange(NIT, dtype=np.float32)
    q = (2.0 ** -(k + 2)).astype(np.float32)
    c["c_bis"] = np.ascontiguousarray(np.broadcast_to(np.concatenate([q, 2 * q])[None, :], (128, 2 * NIT))).astype(np.float32)
    sel = np.zeros((65, 64), np.float32)
    sel[64, :] = 1.0
    c["c_sel"] = sel
    return c


WNAMES = ["ln1_g", "ln1_b", "ffn1_wg", "ffn1_wu", "ffn1_wd", "w_in", "gm_ln_g", "gm_ln_b", "gm_ws", "gm_bs",
          "w_br_a", "w_br_b", "w_out", "ln2_g", "ln2_b", "ffn2_wg", "ffn2_wu", "ffn2_wd", "ln3_g", "ln3_b"]
WSHAPES = {"ln1_g": [1, D], "ln1_b": [1, D], "ffn1_wg": [D, DFF], "ffn1_wu": [D, DFF], "ffn1_wd": [DFF, D],
           "w_in": [D, INW], "gm_ln_g": [1, D], "gm_ln_b": [1, D], "gm_ws": [4, 128, 128], "gm_bs": [4, 128],
           "w_br_a": [D, D], "w_br_b": [D, D], "w_out": [D, D], "ln2_g": [1, D], "ln2_b": [1, D],
           "ffn2_wg": [D, DFF], "ffn2_wu": [D, DFF], "ffn2_wd": [DFF, D], "ln3_g": [1, D], "ln3_b": [1, D]}


def build_nc(S, do_sample=True, nit=NIT, stage=99):
    assert S % TT == 0
    NTILE = S // TT
    TOPK = float(min(256, S // 4))
    nc = bass.Bass("TRN2", target_bir_lowering=False)
    es = ExitStack()

    def din(name, shape):
        return nc.dram_tensor(name, shape, F32, kind="ExternalInput").ap()

    def dout(name, shape):
        return nc.dram_tensor(name, shape, F32, kind="ExternalOutput").ap()

    def dscr(name, shape, dt=BF16):
        return nc.dram_tensor(name, shape, dt, kind="Internal").ap()

    xp = din("xp", [S, D])
    xs = din("xs", [64, D])
    ck = din("ck", [NSTREAM, PAST, 256])
    cv = din("cv", [NSTREAM, PAST, 256])
    cki = din("cki", [NSTREAM, PAST, 64])
    relt = din("rel_table", [32, 16])
    W = {n: din(n, WSHAPES[n]) for n in WNAMES}
    C = {n: din(n, list(v.shape)) for n, v in host_consts().items()}

    yp = dout("yp", [S, D]); ys = dout("ys", [64, D])
    kp = dout("kp", [S, 256]); vp = dout("vp", [S, 256]); kip = dout("kip", [S, 64])
    ks = dout("ks", [64, 256]); vs = dout("vs", [64, 256]); kis = dout("kis", [64, 64])
    gvs = dout("gvs", [64, D])

    wb = {}
    for n in ["ffn1_wg", "ffn1_wu", "ffn1_wd", "w_in", "w_br_a", "w_br_b", "w_out", "ffn2_wg", "ffn2_wu", "ffn2_wd"]:
        wb[n] = dscr(n + "_b", WSHAPES[n])
    b_wb = {n: Buf(n) for n in wb}
    ktS = dscr("ktS", [128, 2, S]); kiS = dscr("kiS", [128, S]); v1S = dscr("v1S", [S, 264])
    ktX = dscr("ktX", [NSTREAM, 128, 2, LX]); kiX = dscr("kiX", [NSTREAM, 128, LX]); v1X = dscr("v1X", [NSTREAM, LX, 264])
    gtab = dscr("gtab", [16, 384], F32)
    wfm = dscr("wfm", [15, 128, 1024])
    b_wfm = Buf("wfm")
    wsd = dscr("wsd", [4, 128, 128], F32)

    with es:
        fw = FW(nc, es)

        cur = {"st": es}

        def sb(name, shape, dt=F32):
            return cur["st"].enter_context(nc.sbuf_tensor(name, shape, dt))

        psall = es.enter_context(nc.psum_tensor("psall", [128, 4096], F32))
        banks = [psall[:, 512 * i:512 * (i + 1)] for i in range(8)]
        banks_bf = [b.bitcast(BF16) for b in banks]
        b_bank = [Buf("bank%d" % i, excl=True) for i in range(8)]
        rr = {"list": [4, 5, 6, 7], "i": 0}

        def set_rr(lst):
            rr["list"] = lst
            rr["i"] = 0

        def nb():
            i = rr["list"][rr["i"] % len(rr["list"])]
            rr["i"] += 1
            return i

        ident_f = sb("ident_f", [128, 128]); ident_b = sb("ident_b", [128, 128], BF16)
        exch = sb("exch", [128, 128]); triu = sb("triu", [128, 128])
        bis = sb("bis", [128, 2 * NIT]); selr = sb("selr", [65, 64])
        b_const = Buf("const")
        xres_sets = [sb("xresA", [128, 2, D]), sb("xresB", [128, 2, D])]
        b_xres_sets = [[Buf("xa0"), Buf("xa1")], [Buf("xb0"), Buf("xb1")]]
        xres = xres_sets[0]; b_xres = b_xres_sets[0]
        actT = sb("actT", [128, 8, TT], BF16); b_actT = Buf("actT")
        lnpool = sb("lnpool", [128, 4 * D])
        lnA = lnpool[:, 0:D]; b_lnA = Buf("lnA")
        lnB = lnpool[:, D:2 * D]; b_lnB = Buf("lnB")
        gam = lnpool[:, 2 * D:3 * D]; b_gam = Buf("gam")
        bet = lnpool[:, 3 * D:4 * D]; b_bet = Buf("bet")
        junks = [lnpool[:, 0:2 * D].bitcast(U8), lnpool[:, 2 * D:4 * D].bitcast(U8)]
        b_junks = [[b_lnA, b_lnB], [b_gam, b_bet]]
        stat = sb("stat", [128, 16]); b_stat = Buf("stat")
        tmpb = [sb("tmpb%d" % i, [128, 512], BF16) for i in range(2)]; b_tmpb = [Buf(), Buf()]
        tmpf = [sb("tmpf%d" % i, [128, 512]) for i in range(2)]; b_tmpf = [Buf(), Buf()]
        bq = sb("bq", [128, 2 * NIT + 8]); b_bq = Buf("bq")
        epsc = sb("epsc", [128, 2])
        LMAX = 8192 if S > 1152 else max(S, LX)
        hS = dscr("hS", [S + 64, D], F32); h2S = dscr("h2S", [S + 64, D], F32)
        b_hS = [Buf() for _ in range(NTILE + 1)]; b_h2S = [Buf() for _ in range(NTILE + 1)]

        def wload(src_ap, nparts, shape_free, reads=()):
            i = wctr["i"] % NSLOT
            wctr["i"] += 1
            n = int(np.prod(shape_free))
            dst = wslots[i][0:nparts, 0:n]
            if len(shape_free) == 2:
                dst = dst.rearrange("p (a b) -> p a b", b=shape_free[1])
            fw.dma("sp", dst, src_ap, reads=list(reads), writes=[b_wslot[i]])
            return dst, b_wslot[i]

        def transpose_to(src_ap_fn, nchunks, ntok, dst_fn, src_bufs, dst_buf):
            c = 0
            while c < nchunks:
                n = min(8, nchunks - c)
                bi = nb()
                pv = banks_bf[bi].rearrange("p (a b) -> p a b", b=128)
                for j in range(n):
                    fw.op("pe", lambda e, j=j, c=c, pv=pv: e.transpose(out=pv[:, j, 0:ntok], in_=src_ap_fn(c + j), identity=ident_b[0:ntok, 0:ntok]),
                          reads=src_bufs + [b_const], writes=[b_bank[bi]], inc=(j == n - 1))
                fw.op("act", lambda e, c=c, n=n, pv=pv: e.copy(out=dst_fn(c, n), in_=pv[:, 0:n, 0:ntok]),
                      reads=[b_bank[bi]], writes=[dst_buf])
                c += n

        def layer_norm(src_ap, ntok, g_ap, b_ap, out_ap, src_bufs, out_bufs, tmp=None, b_tmp=None, epsj=0):
            fw.dma("sp", gam[:], g_ap.to_broadcast([128, D]), writes=[b_gam])
            fw.dma("sp", bet[:], b_ap.to_broadcast([128, D]), writes=[b_bet])
            for h in range(2):
                fw.op("dve", lambda e, h=h: e.bn_stats(out=stat[0:ntok, 6 * h:6 * h + 6], in_=src_ap[:, 512 * h:512 * h + 512]),
                      reads=src_bufs, writes=[b_stat])
            fw.op("dve", lambda e: e.bn_aggr(out=stat[0:ntok, 12:14], in_=stat[0:ntok, 0:12]), reads=[b_stat], writes=[b_stat])
            fw.op("act", lambda e: e.activation(out=stat[0:ntok, 14:15], in_=stat[0:ntok, 13:14], func=AF.Sqrt, bias=epsc[0:ntok, epsj:epsj + 1], scale=1.0),
                  reads=[b_stat, b_const], writes=[b_stat])
            fw.op("dve", lambda e: e.reciprocal(out=stat[0:ntok, 14:15], in_=stat[0:ntok, 14:15]), reads=[b_stat], writes=[b_stat])
            fw.op("dve", lambda e: e.tensor_scalar(out=stat[0:ntok, 15:16], in0=stat[0:ntok, 12:13], scalar1=stat[0:ntok, 14:15], scalar2=-1.0,
                                                   op0=ALU.mult, op1=ALU.mult), reads=[b_stat], writes=[b_stat])
            t = tmp if tmp is not None else lnB
            bt = b_tmp if b_tmp is not None else b_lnB
            fw.op("act", lambda e: e.activation(out=t[0:ntok, :], in_=src_ap, func=AF.Identity, scale=stat[0:ntok, 14:15], bias=stat[0:ntok, 15:16]),
                  reads=src_bufs + [b_stat], writes=[bt])
            fw.op("pool", lambda e: e.tensor_tensor(out=t[0:ntok, :], in0=t[0:ntok, :], in1=gam[0:ntok, :], op=ALU.mult),
                  reads=[bt, b_gam], writes=[bt])
            fw.op("pool", lambda e: e.tensor_tensor(out=out_ap, in0=t[0:ntok, :], in1=bet[0:ntok, :], op=ALU.add),
                  reads=[bt, b_bet], writes=out_bufs)

        def cast_T(src_ap, ntok, blk, src_bufs, dst, dst_buf, col0, nch=8):
            tb = lnA[:].bitcast(BF16)
            fw.op("dve", lambda e: e.tensor_copy(out=tb[0:ntok, 0:nch * 128], in_=src_ap), reads=src_bufs, writes=[b_lnA])
            transpose_to(lambda c: tb[0:ntok, c * 128:(c + 1) * 128], nch, ntok,
                         lambda c0, n: dst[:, c0:c0 + n, col0:col0 + ntok], [b_lnA], dst_buf)

        def ffn(blocks):
            set_rr([4, 5, 6, 7])
            nchunks = [(n0, min(512, DFF - n0)) for n0 in range(0, DFF, 512)]
            for (n0, ncol) in nchunks:
                for bidx, (ntok, col0) in enumerate(blocks):
                    gb, ub = nb(), nb()
                    for wres, bnk in ((wgR, gb), (wuR, ub)):
                        for kc in range(8):
                            fw.op("pe", lambda e, wres=wres, kc=kc, bnk=bnk, ntok=ntok, col0=col0, ncol=ncol, n0=n0:
                                  e.matmul(out=banks[bnk][0:ntok, 0:ncol], lhsT=actT[:, kc, col0:col0 + ntok], rhs=wres[:, kc, n0:n0 + ncol],
                                           start=(kc == 0), stop=(kc == 7)),
                                  reads=[b_actT, b_wres], writes=[b_bank[bnk]], inc=(kc == 7))
                    ti = (bidx + n0 // 512) % 2
                    fw.op("act", lambda e, ti=ti, gb=gb, ntok=ntok, ncol=ncol: e.activation(out=tmpb[ti][0:ntok, 0:ncol], in_=banks[gb][0:ntok, 0:ncol], func=AF.Silu),
                          reads=[b_bank[gb]], writes=[b_tmpb[ti]])
                    hb = b_hidblk[bidx]
                    fw.op("dve", lambda e, ti=ti, ub=ub, ntok=ntok, ncol=ncol, n0=n0, bidx=bidx:
                          e.tensor_tensor(out=hidblk[bidx][0:ntok, n0:n0 + ncol], in0=banks[ub][0:ntok, 0:ncol], in1=tmpb[ti][0:ntok, 0:ncol], op=ALU.mult),
                          reads=[b_bank[ub], b_tmpb[ti]], writes=[hb])
            for bidx, (ntok, col0) in enumerate(blocks):
                transpose_to(lambda c, bidx=bidx, ntok=ntok: hidblk[bidx][0:ntok, c * 128:(c + 1) * 128], 22, ntok,
                             lambda c0, n, col0=col0, ntok=ntok: hidT[:, c0:c0 + n, col0:col0 + ntok], [b_hidblk[bidx]], b_hidT)
            for kc in range(22):
                for bidx, (ntok, col0) in enumerate(blocks):
                    for hf in range(2):
                        bnk = 2 * bidx + hf
                        fw.op("pe", lambda e, kc=kc, bnk=bnk, ntok=ntok, col0=col0, hf=hf:
                              e.matmul(out=banks[bnk][0:ntok, :], lhsT=hidT[:, kc, col0:col0 + ntok], rhs=wdR[:, kc, hf * 512:(hf + 1) * 512],
                                       start=(kc == 0), stop=(kc == 21)),
                              reads=[b_hidT, b_wres], writes=[b_bank[bnk]], inc=(kc == 21))

        def ffn_phase(which, in_ap_fn, in_bufs_fn, out_ap_fn, out_bufs_fn, g_ap, b_ap, wnames):
            nonlocal wgR, wuR, wdR, hidT, hidblk, b_hidblk, b_hidT, b_wres, xres, b_xres
            ph = ExitStack()
            cur["st"] = ph
            wgR = sb("wgR%d" % which, [128, 8, DFF], BF16); wuR = sb("wuR%d" % which, [128, 8, DFF], BF16)
            wdR = sb("wdR%d" % which, [128, 22, D], BF16)
            hidT = sb("hidT%d" % which, [128, 22, TT], BF16); b_hidT = Buf("hidT")
            hidblk = [sb("hid%d_%d" % (which, i), [128, DFF], BF16) for i in range(2)]
            b_hidblk = [Buf("hida"), Buf("hidb")]
            cur["st"] = es
            b_wres = Buf("wres")
            for kc in range(8):
                fw.dma("sp", wgR[:, kc, :], wb[wnames[0]][kc * 128:(kc + 1) * 128, :], writes=[b_wres])
                fw.dma("sp", wuR[:, kc, :], wb[wnames[1]][kc * 128:(kc + 1) * 128, :], writes=[b_wres])
            for kc in range(0, 22, 2):
                fw.dma("sp", wdR[:, kc:kc + 2, :], wb[wnames[2]][kc * 128:(kc + 2) * 128, :].rearrange("(a p) n -> p a n", p=128), writes=[b_wres])
            tiles = [(t, [(128, 0), (128, 128)]) for t in range(NTILE)]
            if do_sample:
                tiles.append((NTILE, [(64, 0)]))
            def front(ti_):
                nonlocal xres, b_xres
                t, blocks = tiles[ti_]
                xres = xres_sets[ti_ % 2]; b_xres = b_xres_sets[ti_ % 2]
                for bidx, (ntok, col0) in enumerate(blocks):
                    fw.dma("sp", xres[0:ntok, bidx, :], in_ap_fn(t, bidx, ntok), reads=in_bufs_fn(t), writes=[b_xres[bidx]])
                to_actT(blocks)
            front(0)
            for ti_, (t, blocks) in enumerate(tiles):
                xres = xres_sets[ti_ % 2]; b_xres = b_xres_sets[ti_ % 2]
                ffn(blocks)
                if ti_ + 1 < len(tiles):
                    front(ti_ + 1)
                    xres = xres_sets[ti_ % 2]; b_xres = b_xres_sets[ti_ % 2]
                resid_ln(blocks, g_ap, b_ap, final_out=lambda bidx, ntok, t=t: out_ap_fn(t, bidx, ntok), out_bufs=out_bufs_fn(t))
            fw.barrier()
            fw.emit()
            ph.close()

        wgR = wuR = wdR = hidT = hidblk = b_hidblk = b_hidT = b_wres = None

        def resid_ln(blocks, g_ap, b_ap, final_out=None, out_bufs=()):
            for bidx, (ntok, col0) in enumerate(blocks):
                for hf in range(2):
                    bnk = 2 * bidx + hf
                    fw.op("dve", lambda e, bnk=bnk, hf=hf, ntok=ntok, bidx=bidx, xr=xres: e.scalar_tensor_tensor(
                        out=lnA[0:ntok, hf * 512:(hf + 1) * 512], in0=banks[bnk][0:ntok, :], scalar=float(0.5 / ALPHA),
                        in1=xr[0:ntok, bidx, hf * 512:(hf + 1) * 512], op0=ALU.mult, op1=ALU.add),
                        reads=[b_bank[bnk], b_xres[bidx]], writes=[b_lnA])
                layer_norm(lnA[0:ntok, :], ntok, g_ap, b_ap, xres[0:ntok, bidx, :], [b_lnA], [b_xres[bidx]], epsj=1)
                if final_out is not None:
                    fw.dma("sp", final_out(bidx, ntok), xres[0:ntok, bidx, :], reads=[b_xres[bidx]], writes=list(out_bufs))

        def mixed_resid_ln(blocks, g_ap, b_ap):
            for bidx, (ntok, col0) in enumerate(blocks):
                for hf in range(2):
                    bnk = 2 * bidx + hf
                    fw.op("dve", lambda e, bnk=bnk, hf=hf, ntok=ntok, bidx=bidx, xr=xres: e.scalar_tensor_tensor(
                        out=lnA[0:ntok, hf * 512:(hf + 1) * 512], in0=xr[0:ntok, bidx, hf * 512:(hf + 1) * 512], scalar=ALPHA,
                        in1=banks[bnk][0:ntok, :], op0=ALU.mult, op1=ALU.add),
                        reads=[b_bank[bnk], b_xres[bidx]], writes=[b_lnA])
                layer_norm(lnA[0:ntok, :], ntok, g_ap, b_ap, xres[0:ntok, bidx, :], [b_lnA], [b_xres[bidx]])

        def to_actT(blocks):
            for bidx, (ntok, col0) in enumerate(blocks):
                cast_T(xres[0:ntok, bidx, :], ntok, bidx, [b_xres[bidx]], actT, b_actT, col0)

        winb = wb["w_in"]
        b_win = b_wb["w_in"]

        def win_feature_major(blocks, Ttok, out_k, out_ki):
            set_rr([4, 5, 6, 7])
            specs = [("q", c, c * 128, qT[:, c, 0:Ttok], b_qT) for c in range(8)]
            specs += [("k", p, 1024 + p * 128, ktile[:, p, 0:Ttok], b_ktile) for p in range(2)]
            specs += [("qi", j, 1536 + j * 128, qiT[:, j, 0:Ttok], b_qiT) for j in range(4)]
            specs += [("ki", 0, 2048, kitile[:, 0:Ttok], b_kitile)]
            for si_, (kind, idx, c0, dst, dbuf) in enumerate(specs):
                ap_, b_ = wload(wfm[si_, :, :].rearrange("p (a b) -> p a b", b=128), 128, [8, 128], reads=[b_wfm])
                bnk = nb()
                for kc in range(8):
                    fw.op("pe", lambda e, ap_=ap_, kc=kc, bnk=bnk: e.matmul(out=banks[bnk][:, 0:Ttok], lhsT=ap_[:, kc, :], rhs=actT[:, kc, 0:Ttok],
                                                                             start=(kc == 0), stop=(kc == 7)),
                          reads=[b_actT, b_], writes=[b_bank[bnk]], inc=(kc == 7))
                fw.op("act", lambda e, dst=dst, bnk=bnk: e.copy(out=dst, in_=banks[bnk][:, 0:Ttok]), reads=[b_bank[bnk]], writes=[dbuf])
            out_k()
            out_ki()

        def win_token_major(blocks, out_kv, out_kis, gv_out=None):
            set_rr([4, 5, 6, 7])
            units = [wload(winb[kh * 512:(kh + 1) * 512, 1024:1536].rearrange("(kc p) n -> p kc n", p=128), 128, [4, 512]) for kh in range(2)]
            units2 = wload(winb[:, 2048:2120].rearrange("(kc p) n -> p kc n", p=128), 128, [8, 72])
            for bidx, (ntok, col0) in enumerate(blocks):
                b1, b2 = nb(), nb()
                for kc in range(8):
                    ap_, b_ = units[kc // 4]
                    fw.op("pe", lambda e, ap_=ap_, kc=kc, b1=b1, ntok=ntok, col0=col0: e.matmul(out=banks[b1][0:ntok, :], lhsT=actT[:, kc, col0:col0 + ntok], rhs=ap_[:, kc % 4, :],
                                                                                                 start=(kc == 0), stop=(kc == 7)),
                          reads=[b_actT, b_], writes=[b_bank[b1]], inc=(kc == 7))
                for kc in range(8):
                    ap_, b_ = units2
                    fw.op("pe", lambda e, ap_=ap_, kc=kc, b2=b2, ntok=ntok, col0=col0: e.matmul(out=banks[b2][0:ntok, 0:72], lhsT=actT[:, kc, col0:col0 + ntok], rhs=ap_[:, kc, :],
                                                                                                 start=(kc == 0), stop=(kc == 7)),
                          reads=[b_actT, b_], writes=[b_bank[b2]], inc=(kc == 7))
                fw.op("act", lambda e, bidx=bidx, b1=b1, ntok=ntok: e.copy(out=stg[0:ntok, bidx, 0:512], in_=banks[b1][0:ntok, :]), reads=[b_bank[b1]], writes=[b_stg[bidx]])
                fw.op("act", lambda e, bidx=bidx, b2=b2, ntok=ntok: e.copy(out=stg[0:ntok, bidx, 512:584], in_=banks[b2][0:ntok, 0:72]), reads=[b_bank[b2]], writes=[b_stg[bidx]])
                if 'A' not in DBG:
                    fw.op("dve", lambda e, bidx=bidx, b1=b1, ntok=ntok: e.tensor_copy(out=v1stg[0:ntok, bidx, :, 0:64], in_=stg[0:ntok, bidx, 256:512].rearrange("p (a b) -> p a b", b=64)),
                          reads=[b_stg[bidx], b_const], writes=[b_v1stg[bidx]])
                if 'B' not in DBG:
                    out_kv(bidx, ntok)
                if 'C' not in DBG:
                    out_kis(bidx, ntok)
            if stage < 3.4:
                return
            for ch in range(4):
                c0 = 2120 + ch * 512
                units = [wload(winb[kh * 512:(kh + 1) * 512, c0:c0 + 512].rearrange("(kc p) n -> p kc n", p=128), 128, [4, 512]) for kh in range(2)]
                for bidx, (ntok, col0) in enumerate(blocks):
                    b1 = nb()
                    for kc in range(8):
                        ap_, b_ = units[kc // 4]
                        fw.op("pe", lambda e, ap_=ap_, kc=kc, b1=b1, ntok=ntok, col0=col0: e.matmul(out=banks[b1][0:ntok, :], lhsT=actT[:, kc, col0:col0 + ntok], rhs=ap_[:, kc % 4, :],
                                                                                                     start=(kc == 0), stop=(kc == 7)),
                              reads=[b_actT, b_], writes=[b_bank[b1]], inc=(kc == 7))
                    if ch < 2:
                        fw.op("act", lambda e, b1=b1, ntok=ntok, bidx=bidx, ch=ch: e.activation(out=ubuf[0:ntok, bidx, ch * 512:(ch + 1) * 512], in_=banks[b1][0:ntok, :], func=AF.Gelu_apprx_tanh),
                              reads=[b_bank[b1]], writes=[b_u[bidx]])
                    else:
                        fw.op("act", lambda e, b1=b1, ntok=ntok, bidx=bidx, ch=ch: e.activation(out=mbuf[0:ntok, bidx, (ch - 2) * 512:(ch - 1) * 512], in_=banks[b1][0:ntok, :], func=AF.Gelu_apprx_tanh),
                              reads=[b_bank[b1]], writes=[b_m])
            if stage < 3.6:
                return
            for bidx, (ntok, col0) in enumerate(blocks):
                layer_norm(mbuf[0:ntok, bidx, :], ntok, W["gm_ln_g"][0:1, :], W["gm_ln_b"][0:1, :], lnA[0:ntok, :], [b_m], [b_lnA])
                fw.op("dve", lambda e, ntok=ntok, bidx=bidx: e.tensor_copy(out=vgb[0:ntok, bidx, :], in_=lnA[0:ntok, :]), reads=[b_lnA], writes=[b_vg[bidx]])
                if gv_out is not None:
                    fw.dma("sp", gv_out[0:ntok, :], lnA[0:ntok, :], reads=[b_lnA], writes=[])

        def run_interleaved(gens):
            alive = list(gens)
            while alive:
                for g in list(alive):
                    try:
                        next(g)
                    except StopIteration:
                        alive.remove(g)

        def attention(par, nq, qc0, hT_cols, key_src, nkb, diag_fill, bias_type, topk):
            L = nkb * 128
            Ssb = S2[par]; b_S = b_S2[par]
            bq = bqs[par]; b_bq = b_bqs[par]
            bq2 = bq2s[par]; b_bq2 = b_bq2s[par]
            wiq = wiqs[par]; b_wiq = b_wiqs[par]
            thrb = thrbs[par]; b_thrb = b_thrbs[par]
            thrB = thrBs[par]; b_thrB = b_thrBs[par]
            jk = junks[par]; b_jk = b_junks[par]
            u2 = wload(winb[:, 2112:2120].rearrange("(kc p) n -> p kc n", p=128), 128, [8, 8])
            bnk = nb()
            for kc in range(8):
                fw.op("pe", lambda e, kc=kc, bnk=bnk: e.matmul(out=banks[bnk][0:nq, 0:8], lhsT=actT[:, kc, hT_cols:hT_cols + nq], rhs=u2[0][:, kc, :],
                                                                start=(kc == 0), stop=(kc == 7)),
                      reads=[b_actT, u2[1]], writes=[b_bank[bnk]], inc=(kc == 7))
            fw.op("act", lambda e, bnk=bnk: e.activation(out=wiq[0:nq, 0:8], in_=banks[bnk][0:nq, 0:8], func=AF.Abs, scale=float(64 ** -0.5 * 8 ** -0.5)),
                  reads=[b_bank[bnk]], writes=[b_wiq])
            fw.op("dve", lambda e, bnk=bnk: e.tensor_scalar(out=wiq[0:nq, 8:16], in0=banks[bnk][0:nq, 0:8], scalar1=0.0, scalar2=2.0,
                                                            op0=ALU.is_ge, op1=ALU.mult), reads=[b_bank[bnk]], writes=[b_wiq])
            fw.op("dve", lambda e: e.tensor_scalar(out=wiq[0:nq, 8:16], in0=wiq[0:nq, 8:16], scalar1=-1.0, scalar2=None, op0=ALU.add), reads=[b_wiq], writes=[b_wiq])
            if KATT < 1:
                return
            for k0 in range(0, L, 512):
                ncol = min(512, L - k0)
                ci = (k0 // 512) % 2
                src, sbufs = key_src("ki", k0, ncol)
                fw.dma("sp", kic[ci][:, 0:ncol], src, reads=sbufs, writes=[b_kic[ci]])
                for h in range(8):
                    bnk = nb()
                    hp = (h % 2) * 64
                    fw.op("pe", lambda e, h=h, hp=hp, bnk=bnk, ci=ci, ncol=ncol: e.matmul(out=banks[bnk][0:nq, 0:ncol], lhsT=qiT[hp:hp + 64, h // 2, qc0:qc0 + nq],
                                                                                           rhs=kic[ci][hp:hp + 64, 0:ncol], start=True, stop=True),
                          reads=[b_qiT, b_kic[ci]], writes=[b_bank[bnk]])
                    ti = h % 2
                    fw.op("act", lambda e, h=h, bnk=bnk, ti=ti, ncol=ncol: e.activation(out=tmpf[ti][0:nq, 0:ncol], in_=banks[bnk][0:nq, 0:ncol], func=AF.Relu,
                                                                                         scale=wiq[0:nq, h:h + 1]),
                          reads=[b_bank[bnk], b_wiq], writes=[b_tmpf[ti]])
                    if h == 0:
                        fw.op("dve", lambda e, ti=ti, k0=k0, ncol=ncol: e.tensor_scalar(out=Ssb[0:nq, k0:k0 + ncol], in0=tmpf[ti][0:nq, 0:ncol], scalar1=wiq[0:nq, 8:9], scalar2=None, op0=ALU.mult),
                              reads=[b_tmpf[ti], b_wiq], writes=[b_S])
                    else:
                        fw.op("dve", lambda e, h=h, ti=ti, k0=k0, ncol=ncol: e.scalar_tensor_tensor(out=Ssb[0:nq, k0:k0 + ncol], in0=tmpf[ti][0:nq, 0:ncol], scalar=wiq[0:nq, 8 + h:9 + h],
                                                                                                    in1=Ssb[0:nq, k0:k0 + ncol], op0=ALU.mult, op1=ALU.add),
                              reads=[b_tmpf[ti], b_wiq, b_S], writes=[b_S])
            if KATT < 2:
                return
            NQ = 2 * NIT
            fw.op("dve", lambda e: e.tensor_reduce(out=bq[0:nq, NQ:NQ + 1], in_=Ssb[0:nq, 0:L], axis=AX.X, op=ALU.max), reads=[b_S], writes=[b_bq])
            fw.op("dve", lambda e: e.tensor_reduce(out=bq[0:nq, NQ + 1:NQ + 2], in_=Ssb[0:nq, 0:L], axis=AX.X, op=ALU.min), reads=[b_S], writes=[b_bq])
            if diag_fill is not None:
                for (r1, c0_, c1_) in diag_fill:
                    fw.op("dve", lambda e, r1=r1, c0_=c0_, c1_=c1_: e.memset(Ssb[0:r1, c0_:c1_], NEG), reads=[b_bq], writes=[b_S])
            fw.op("dve", lambda e: e.tensor_tensor(out=bq[0:nq, NQ + 2:NQ + 3], in0=bq[0:nq, NQ:NQ + 1], in1=bq[0:nq, NQ + 1:NQ + 2], op=ALU.subtract), reads=[b_bq], writes=[b_bq])
            fw.op("dve", lambda e: e.scalar_tensor_tensor(out=bq[0:nq, NQ + 1:NQ + 2], in0=bq[0:nq, NQ + 2:NQ + 3], scalar=-0.01, in1=bq[0:nq, NQ + 1:NQ + 2], op0=ALU.mult, op1=ALU.add),
                  reads=[b_bq], writes=[b_bq])
            fw.op("dve", lambda e: e.tensor_scalar(out=bq[0:nq, NQ + 1:NQ + 2], in0=bq[0:nq, NQ + 1:NQ + 2], scalar1=-1e-6, scalar2=None, op0=ALU.add), reads=[b_bq], writes=[b_bq])
            fw.op("dve", lambda e: e.tensor_scalar(out=bq[0:nq, NQ + 2:NQ + 3], in0=bq[0:nq, NQ + 2:NQ + 3], scalar1=1.02, scalar2=2e-6, op0=ALU.mult, op1=ALU.add), reads=[b_bq], writes=[b_bq])
            fw.op("dve", lambda e: e.tensor_scalar(out=bq[0:nq, 0:NQ], in0=bis[0:nq, :], scalar1=bq[0:nq, NQ + 2:NQ + 3], scalar2=None, op0=ALU.mult), reads=[b_bq, b_const], writes=[b_bq])
            fw.op("dve", lambda e: e.scalar_tensor_tensor(out=bq[0:nq, NQ + 3:NQ + 4], in0=bq[0:nq, NQ + 2:NQ + 3], scalar=0.5, in1=bq[0:nq, NQ + 1:NQ + 2], op0=ALU.mult, op1=ALU.add),
                  reads=[b_bq], writes=[b_bq])
            mid = bq[0:nq, NQ + 3:NQ + 4]
            cnt = bq[0:nq, NQ + 4:NQ + 5]
            t2 = bq[0:nq, NQ + 5:NQ + 6]
            Ld = L if L < 1024 else int(round(0.46 * L / 64.0)) * 64
            nA = L - Ld
            yield
            b_jkD = Buf("jkD"); b_jkA = Buf("jkA"); b_cnt = Buf("cnt")
            cntv = bq2[0:nq, 1:2]
            for k in range(nit):
                edge = list(b_jk) if (k == 0 or k == nit - 1) else []
                if nA > 0:
                    fw.op("act", lambda e: e.activation(out=jk[0:nq, Ld:L], in_=Ssb[0:nq, Ld:L], func=AF.Sign, bias=mid, scale=-1.0, accum_out=bq2[0:nq, 0:1]),
                          reads=[b_S, b_bq], writes=[b_jkA, b_bq2] + edge)
                fw.op("dve", lambda e: e.tensor_scalar(out=jk[0:nq, 0:Ld], in0=Ssb[0:nq, 0:Ld], scalar1=mid, scalar2=0.0, op0=ALU.is_ge, op1=ALU.add, accum_out=cntv),
                      reads=[b_S, b_bq], writes=[b_jkD, b_cnt] + edge)
                yield
                if nA > 0:
                    fw.op("dve", lambda e: e.scalar_tensor_tensor(out=cntv, in0=bq2[0:nq, 0:1], scalar=-0.5, in1=cntv, op0=ALU.mult, op1=ALU.add), reads=[b_cnt, b_bq2], writes=[b_cnt])
                    yield
                fw.op("dve", lambda e, k=k: e.tensor_scalar(out=t2, in0=cntv, scalar1=float(topk - 0.5 * nA), scalar2=bq[0:nq, NIT + k:NIT + k + 1], op0=ALU.is_ge, op1=ALU.mult), reads=[b_cnt, b_bq], writes=[b_cnt])
                yield
                fw.op("dve", lambda e, k=k: e.tensor_scalar(out=mid, in0=mid, scalar1=t2, scalar2=bq[0:nq, k:k + 1], op0=ALU.add, op1=ALU.subtract), reads=[b_cnt, b_bq], writes=[b_bq])
                yield
            set_rr([4, 5, 6, 7])
            if KATT < 3:
                return
            fw.op("dve", lambda e: e.tensor_tensor(out=bq[0:nq, NQ + 6:NQ + 7], in0=mid, in1=bq[0:nq, nit - 1:nit], op=ALU.subtract), reads=[b_bq], writes=[b_bq])
            fw.op("dve", lambda e: e.tensor_copy(out=thrb[0:nq, :], in_=bq[0:nq, NQ + 6:NQ + 7].to_broadcast([nq, 128])), reads=[b_bq], writes=[b_thrb])
            bnk = nb()
            fw.op("pe", lambda e, bnk=bnk: e.transpose(out=banks[bnk][:, 0:nq], in_=thrb[0:nq, :], identity=ident_f[0:nq, 0:nq]), reads=[b_thrb, b_const], writes=[b_bank[bnk]])
            fw.op("act", lambda e, bnk=bnk: e.copy(out=thrB[:, 0:nq], in_=banks[bnk][:, 0:nq]), reads=[b_bank[bnk]], writes=[b_thrB])
            if KATT < 4:
                return
            for b4 in range(4):
                fw.op("dve", lambda e, b4=b4: e.memset(banks[b4][0:65, :], 0.0), reads=[], writes=[b_bank[b4]])
            ptc = {"i": 0, "g": 0}
            PAIRS = [(4, 5), (6, 7)]

            def prep(kb):
                g4 = kb // 4
                if kb % 4 == 0:
                    n4 = min(4, nkb - kb)
                    ci = g4 % 2
                    src, sbufs = key_src("kt", kb * 128, n4 * 128)
                    fw.dma("sp", ktc[ci][:, :, 0:n4 * 128], src, reads=sbufs, writes=[b_ktc[ci]])
                    src, sbufs = key_src("v1", kb * 128, n4 * 128)
                    fw.dma("sp", v1c[ci][:, 0:n4, :], src, reads=sbufs, writes=[b_v1c[ci]])
                    bnk = PAIRS[ptc["g"] % 2][0]
                    pv = banks[bnk][:].rearrange("p (a b) -> p a b", b=128)
                    for j in range(n4):
                        fw.op("pe", lambda e, j=j, kb=kb, pv=pv: e.transpose(out=pv[:, j, 0:nq], in_=Ssb[0:nq, (kb + j) * 128:(kb + j + 1) * 128], identity=ident_f[0:nq, 0:nq]),
                              reads=[b_S, b_const], writes=[b_bank[bnk]], inc=(j == n4 - 1))
                    fw.op("dve", lambda e, ci=ci, pv=pv, n4=n4: e.tensor_tensor(out=mT[ci][:, 0:n4, 0:nq], in0=pv[:, 0:n4, 0:nq], in1=thrB[:, 0:nq].unsqueeze(1).to_broadcast([128, n4, nq]), op=ALU.is_ge),
                          reads=[b_bank[bnk], b_thrB], writes=[b_mT[ci]])

            def emit_qk(kb, G):
                ci = (kb // 4) % 2
                kj = kb % 4
                bAB = PAIRS[ptc["g"] % 2]
                ptc["g"] += 1
                lgs = [banks[b_][:].rearrange("p (a b) -> p a b", b=128) for b_ in bAB]
                if nq == 128:
                    for u in range(2):
                        fw.op("pe", lambda e, u=u, G=G, ci=ci, kj=kj, lgs=lgs: e.matmul(
                            out=lgs[u][:, 0:4, 0:nq], lhsT=ktc[ci][u * 64:u * 64 + 64, G, kj * 128:(kj + 1) * 128],
                            rhs=qT[u * 64:u * 64 + 64, 4 * G:4 * G + 4, qc0:qc0 + nq], start=True, stop=True),
                            reads=[b_ktc[ci], b_qT], writes=[b_bank[bAB[u]]])
                else:
                    for i in range(4):
                        c = 4 * G + i
                        for u in range(2):
                            fw.op("pe", lambda e, i=i, c=c, u=u, G=G, ci=ci, kj=kj, lgs=lgs: e.matmul(
                                out=lgs[u][:, i, 0:nq], lhsT=ktc[ci][u * 64:u * 64 + 64, G, kj * 128:(kj + 1) * 128],
                                rhs=qT[u * 64:u * 64 + 64, c, qc0:qc0 + nq], start=True, stop=True),
                                reads=[b_ktc[ci], b_qT], writes=[b_bank[bAB[u]]], inc=(i == 3))
                return bAB, lgs

            def emit_rest(kb, G, bAB, lgs):
                ci = (kb // 4) % 2
                kj = kb % 4
                bt = bias_type(kb)
                pti = ptc["i"]
                PB = ptb[pti % 2]
                bP = b_ptb[pti % 2]
                ptc["i"] = pti + 1
                if nq == 128 and bt is None:
                    lg2 = psall[:, 512 * bAB[0]:512 * bAB[0] + 1024].rearrange("p (a b) -> p a b", b=128)
                    fw.op("act", lambda e, lg2=lg2, PB=PB: e.activation(out=PB[:, :, :], in_=lg2[:, :, :], func=AF.Exp, scale=0.125),
                          reads=[b_bank[bAB[0]], b_bank[bAB[1]]], writes=[bP])
                    fw.op("dve", lambda e, PB=PB, ci=ci, kj=kj: e.tensor_tensor(out=PB[:, :, :], in0=PB[:, :, :], in1=mT[ci][:, kj:kj + 1, :].to_broadcast([128, 8, 128]), op=ALU.mult),
                          reads=[bP, b_mT[ci]], writes=[bP])
                    for u in range(2):
                        hb = 2 * G + u
                        accv = banks[hb][:].rearrange("p (a b) -> p a b", b=128)
                        fw.op("pe", lambda e, hb=hb, u=u, PB=PB, ci=ci, kj=kj, accv=accv: e.matmul(out=accv[0:65, 0:4, :], lhsT=v1c[ci][:, kj, hb * 66:hb * 66 + 65], rhs=PB[:, 4 * u:4 * u + 4, :],
                                                                                               start=False, stop=False, skip_group_check=True),
                              reads=[b_v1c[ci], bP], writes=[b_bank[hb]])
                    return
                for u in range(2):
                    hb = 2 * G + u
                    bnk = bAB[u]
                    lg = lgs[u]
                    P = PB[:, 4 * u:4 * u + 4, :]
                    if bt is None:
                        fw.op("act", lambda e, lg=lg, P=P: e.activation(out=P[:, :, 0:nq], in_=lg[:, :, 0:nq], func=AF.Exp, scale=0.125), reads=[b_bank[bnk]], writes=[bP])
                    else:
                        tf = tmpf[hb % 2][:].rearrange("p (a b) -> p a b", b=128)
                        fw.op("dve", lambda e, lg=lg, tf=tf, bt=bt, hb=hb: e.scalar_tensor_tensor(out=tf[:, :, 0:nq], in0=lg[:, :, 0:nq], scalar=0.125, in1=biasT[:, bt, 4 * hb:4 * hb + 4, 0:nq],
                                                                                                  op0=ALU.mult, op1=ALU.add),
                              reads=[b_bank[bnk], b_const], writes=[b_tmpf[hb % 2]])
                        fw.op("act", lambda e, tf=tf, P=P: e.activation(out=P[:, :, 0:nq], in_=tf[:, :, 0:nq], func=AF.Exp), reads=[b_tmpf[hb % 2]], writes=[bP])
                    fw.op("dve", lambda e, P=P, ci=ci, kj=kj: e.tensor_tensor(out=P[:, :, 0:nq], in0=P[:, :, 0:nq], in1=mT[ci][:, kj:kj + 1, 0:nq].to_broadcast([128, 4, nq]), op=ALU.mult),
                          reads=[bP, b_mT[ci]], writes=[bP])
                    accv = banks[hb][:].rearrange("p (a b) -> p a b", b=128)
                    if nq == 128:
                        fw.op("pe", lambda e, hb=hb, P=P, ci=ci, kj=kj, accv=accv: e.matmul(out=accv[0:65, 0:4, 0:nq], lhsT=v1c[ci][:, kj, hb * 66:hb * 66 + 65], rhs=P[:, 0:4, 0:nq],
                                                                                            start=False, stop=False, skip_group_check=True),
                              reads=[b_v1c[ci], bP], writes=[b_bank[hb]])
                    else:
                        for j in range(4):
                            fw.op("pe", lambda e, j=j, hb=hb, P=P, ci=ci, kj=kj, accv=accv: e.matmul(out=accv[0:65, j, 0:nq], lhsT=v1c[ci][:, kj, hb * 66:hb * 66 + 65], rhs=P[:, j, 0:nq],
                                                                                                     start=False, stop=False, skip_group_check=True),
                                  reads=[b_v1c[ci], bP], writes=[b_bank[hb]], inc=(j == 3))

            pend = None
            for kb in range(nkb):
                for G in range(2):
                    if G == 0:
                        prep(kb)
                    cur_ = emit_qk(kb, G)
                    if pend is not None:
                        emit_rest(*pend)
                    pend = (kb, G) + cur_
            if pend is not None:
                emit_rest(*pend)
            if KATT < 6:
                return
            for hg in range(4):
                accv = banks[hg][:].rearrange("p (a b) -> p a b", b=128)
                fw.op("act", lambda e, hg=hg, accv=accv: e.copy(out=accS[:, 4 * hg:4 * hg + 4, 0:nq], in_=accv[0:65, :, 0:nq]), reads=[b_bank[hg]], writes=[b_accS])
            fw.op("dve", lambda e: e.reciprocal(out=accS[64:65, :, 0:nq], in_=accS[64:65, :, 0:nq]), reads=[b_accS, b_const], writes=[b_accS])
            for hg in range(4):
                bnk = nb()
                bc = banks[bnk][:].rearrange("p (a b) -> p a b", b=128)
                for j in range(4):
                    fw.op("pe", lambda e, hg=hg, j=j, bc=bc: e.matmul(out=bc[0:64, j, 0:nq], lhsT=selr[64:65, :], rhs=accS[64:65, 4 * hg + j, 0:nq], start=True, stop=True),
                          reads=[b_accS, b_const], writes=[b_bank[bnk]], inc=(j == 3))
                fw.op("dve", lambda e, hg=hg, bc=bc: e.tensor_tensor(out=aT[:, 4 * hg:4 * hg + 4, qc0:qc0 + nq], in0=accS[0:64, 4 * hg:4 * hg + 4, 0:nq], in1=bc[0:64, :, 0:nq], op=ALU.mult),
                      reads=[b_accS, b_bank[bnk]], writes=[b_aT])

        def gmlp_spatial(blocks, sample):
            set_rr([4, 5, 6, 7])
            for bidx, (ntok, col0) in enumerate(blocks):
                b1, b2 = nb(), nb()
                for g in range(4):
                    bnk = b1 if g < 2 else b2
                    lw = wsX[0:ntok, g, 0:ntok] if sample else wsT[:, g, :]
                    fw.op("pe", lambda e, g=g, bnk=bnk, lw=lw, ntok=ntok, bidx=bidx: e.matmul(out=banks[bnk][0:ntok, (g % 2) * 256:(g % 2) * 256 + 256], lhsT=lw,
                                                                                              rhs=vgb[0:ntok, bidx, g * 256:(g + 1) * 256], start=True, stop=True),
                          reads=[b_vg[bidx], b_const], writes=[b_bank[bnk]], inc=(g % 2 == 1))
                tb = lnA[:].bitcast(BF16)
                for g in range(4):
                    bnk = b1 if g < 2 else b2
                    bsc = bsX[0:ntok, g:g + 1] if sample else bsT[0:ntok, g:g + 1]
                    fw.op("dve", lambda e, g=g, bnk=bnk, bsc=bsc, ntok=ntok, bidx=bidx: e.scalar_tensor_tensor(
                        out=tb[0:ntok, g * 256:(g + 1) * 256], in0=banks[bnk][0:ntok, (g % 2) * 256:(g % 2) * 256 + 256], scalar=bsc,
                        in1=ubuf[0:ntok, bidx, g * 256:(g + 1) * 256], op0=ALU.add, op1=ALU.mult),
                        reads=[b_bank[bnk], b_u[bidx], b_const], writes=[b_lnA])
                transpose_to(lambda c, ntok=ntok: tb[0:ntok, c * 128:(c + 1) * 128], 8, ntok,
                             lambda c0, n, col0=col0, ntok=ntok: gT[:, c0:c0 + n, col0:col0 + ntok], [b_lnA], b_gT)

        def proj_tok(blocks, lhs_fn, nk, w_fn, lhs_bufs):
            for k in range(nk):
                ap_, b_ = w_fn(k)
                for bidx, (ntok, col0) in enumerate(blocks):
                    for hf in range(2):
                        bnk = 2 * bidx + hf
                        fw.op("pe", lambda e, k=k, ap_=ap_, bnk=bnk, ntok=ntok, col0=col0, hf=hf: e.matmul(out=banks[bnk][0:ntok, :], lhsT=lhs_fn(k, ntok, col0), rhs=ap_[:, hf * 512:(hf + 1) * 512],
                                                                                                           start=(k == 0), stop=(k == nk - 1)),
                              reads=lhs_bufs + [b_], writes=[b_bank[bnk]], inc=(k == nk - 1 or (bidx == len(blocks) - 1 and hf == 1)))

        def unit_rows(src, r0, nrows, per):
            return wload(src[r0:r0 + per * nrows, :].rearrange("(a p) n -> p a n", p=nrows), nrows, [per, D])

        def merge_and_out(blocks):
            wa = wb["w_br_a"]; wbb = wb["w_br_b"]; wo = wb["w_out"]
            def w_cols(c0):
                cache = {}

                def f(k):
                    if k % 2 == 0:
                        cache["u"] = wload(winb[k * 128:(k + 2) * 128, c0:c0 + D].rearrange("(a p) n -> p a n", p=128), 128, [2, D])
                    ap_, b_ = cache["u"]
                    return ap_[:, k % 2, :], b_
                return f

            def w_rows128(src):
                cache = {}

                def f(k):
                    if k % 2 == 0:
                        cache["u"] = unit_rows(src, k * 128, 128, 2)
                    ap_, b_ = cache["u"]
                    return ap_[:, k % 2, :], b_
                return f

            def w_rows_heads():
                cache = {}

                def f(k):
                    h = hidx_head(k)
                    ap_, b_ = wload(wa[h * 64:(h + 1) * 64, :], 64, [D])
                    return ap_, b_
                return f

            hT_fn = lambda k, ntok, col0: actT[:, k, col0:col0 + ntok]
            set_rr([4, 5, 6, 7])
            proj_tok(blocks, hT_fn, 8, w_cols(4168), [b_actT])
            for bidx, (ntok, col0) in enumerate(blocks):
                for hf in range(2):
                    fw.op("act", lambda e, bidx=bidx, hf=hf, ntok=ntok: e.activation(out=sa[0:ntok, bidx, hf * 512:(hf + 1) * 512], in_=banks[2 * bidx + hf][0:ntok, :], func=AF.Sigmoid),
                          reads=[b_bank[2 * bidx + hf]], writes=[b_sa])
            proj_tok(blocks, lambda k, ntok, col0: aT[:, k, col0:col0 + ntok], 16, w_rows_heads(), [b_aT])
            for bidx, (ntok, col0) in enumerate(blocks):
                for hf in range(2):
                    fw.op("dve", lambda e, bidx=bidx, hf=hf, ntok=ntok: e.tensor_tensor(out=mbuf[0:ntok, bidx, hf * 512:(hf + 1) * 512], in0=banks[2 * bidx + hf][0:ntok, :],
                                                                                         in1=sa[0:ntok, bidx, hf * 512:(hf + 1) * 512], op=ALU.mult),
                          reads=[b_bank[2 * bidx + hf], b_sa], writes=[b_m])
            proj_tok(blocks, hT_fn, 8, w_cols(5192), [b_actT])
            for bidx, (ntok, col0) in enumerate(blocks):
                for hf in range(2):
                    fw.op("act", lambda e, bidx=bidx, hf=hf, ntok=ntok: e.activation(out=sa[0:ntok, bidx, hf * 512:(hf + 1) * 512], in_=banks[2 * bidx + hf][0:ntok, :], func=AF.Sigmoid),
                          reads=[b_bank[2 * bidx + hf]], writes=[b_sa])
            proj_tok(blocks, lambda k, ntok, col0: gT[:, k, col0:col0 + ntok], 8, w_rows128(wbb), [b_gT])
            for bidx, (ntok, col0) in enumerate(blocks):
                for hf in range(2):
                    fw.op("dve", lambda e, bidx=bidx, hf=hf, ntok=ntok: e.tensor_tensor(out=tmpf[hf][0:ntok, :], in0=banks[2 * bidx + hf][0:ntok, :],
                                                                                         in1=sa[0:ntok, bidx, hf * 512:(hf + 1) * 512], op=ALU.mult),
                          reads=[b_bank[2 * bidx + hf], b_sa], writes=[b_tmpf[hf]])
                    fw.op("dve", lambda e, bidx=bidx, hf=hf, ntok=ntok: e.tensor_tensor(out=mbuf[0:ntok, bidx, hf * 512:(hf + 1) * 512], in0=mbuf[0:ntok, bidx, hf * 512:(hf + 1) * 512],
                                                                                         in1=tmpf[hf][0:ntok, :], op=ALU.add),
                          reads=[b_tmpf[hf], b_m], writes=[b_m])
            for bidx, (ntok, col0) in enumerate(blocks):
                cast_T(mbuf[0:ntok, bidx, :], ntok, bidx, [b_m], qT, b_qT, col0)
            proj_tok(blocks, lambda k, ntok, col0: qT[:, k, col0:col0 + ntok], 8, w_rows128(wo), [b_qT])

        for n, t_ in (("c_ident", ident_f), ("c_exch", exch), ("c_triu", triu), ("c_bis", bis), ("c_sel", selr)):
            fw.dma("sp", t_[:], C[n][:, :], writes=[b_const])
        fw.op("dve", lambda e: e.tensor_copy(out=ident_b[:], in_=ident_f[:]), reads=[b_const], writes=[b_const])
        fw.op("dve", lambda e: e.memset(epsc[:, 0:1], 1e-5), reads=[], writes=[b_const])
        fw.op("dve", lambda e: e.memset(epsc[:, 1:2], 1e-5 / (ALPHA * ALPHA)), reads=[b_const], writes=[b_const])
        order = ["ffn1_wg", "ffn1_wu", "ffn1_wd", "w_in", "w_br_a", "w_br_b", "w_out", "ffn2_wg", "ffn2_wu", "ffn2_wd"]
        for n in order:
            src = W[n]
            R = WSHAPES[n][0]
            if n == "w_in":
                for r0 in range(0, R, 256):
                    fw.dma("pool", wb[n][r0:r0 + 256, 1024:INW], src[r0:r0 + 256, 1024:INW], writes=[b_wb[n]])
                for c in range(8):
                    for u, h in enumerate(chunk_heads(c)):
                        fw.dma("pool", wb[n][:, c * 128 + u * 64:c * 128 + u * 64 + 64], src[:, h * 64:(h + 1) * 64], writes=[b_wb[n]])
            else:
                for r0 in range(0, R, 256):
                    fw.dma("pool", wb[n][r0:r0 + 256, :], src[r0:r0 + 256, :], writes=[b_wb[n]])
        fw.op("sp", lambda e: e.nop(), reads=[b_wb[n] for n in order[:3]], writes=[], inc=True)

        def tile_rows(t, bidx, ntok):
            r = (t * TT + bidx * 128) if t < NTILE else S
            return r, r + ntok

        def p1_in(t, bidx, ntok):
            if t < NTILE:
                a, b = tile_rows(t, bidx, ntok)
                return xp[a:b, :]
            return xs[:, :]

        def p1_out(t, bidx, ntok):
            a, b = tile_rows(t, bidx, ntok)
            return hS[a:b, :]
        ffn_phase(1, p1_in, lambda t: [], p1_out, lambda t: [b_hS[t]], W["ln1_g"][0:1, :], W["ln1_b"][0:1, :], ["ffn1_wg", "ffn1_wu", "ffn1_wd"])

        p2 = ExitStack()
        cur["st"] = p2
        wsT = sb("wsT", [128, 4, 128], BF16)
        bsT = sb("bsT", [128, 4])
        wsX = sb("wsX", [64, 4, 64], BF16)
        bsX = sb("bsX", [64, 4])
        biasT = sb("biasT", [128, 2, 16, 128])
        qT = sb("qT", [128, 8, TT], BF16); b_qT = Buf("qT")
        qiT = sb("qiT", [128, 4, TT], BF16); b_qiT = Buf("qiT")
        ktile = sb("ktile", [128, 2, TT], BF16); b_ktile = Buf("ktile")
        kitile = sb("kitile", [128, TT], BF16); b_kitile = Buf("kitile")
        ubuf = sb("ubuf", [128, 2, D], BF16); b_u = [Buf("u0"), Buf("u1")]
        vgb = sb("vgb", [128, 2, D], BF16); b_vg = [Buf("vg0"), Buf("vg1")]
        aT = sb("aT", [64, 16, TT], BF16); b_aT = Buf("aT")
        gT = sb("gT", [128, 8, TT], BF16); b_gT = Buf("gT")
        sa = sb("sa", [128, 2, D], BF16); b_sa = Buf("sa")
        mbuf = sb("mbuf", [128, 2, D]); b_m = Buf("m")
        stg = sb("stg", [128, 2, 512 + 72]); b_stg = [Buf(), Buf()]
        v1stg = sb("v1stg", [128, 2, 4, 66], BF16); b_v1stg = [Buf(), Buf()]
        NSLOT = 4
        wslots = [sb("wslot%d" % i, [128, 2048], BF16) for i in range(NSLOT)]
        b_wslot = [Buf("ws%d" % i) for i in range(NSLOT)]
        wctr = {"i": 0}
        S2 = [sb("Ssb%d" % i, [128, LMAX]) for i in range(2)]; b_S2 = [Buf("S0"), Buf("S1")]
        bqs = [bq, sb("bqb", [128, 2 * NIT + 8])]; b_bqs = [b_bq, Buf("bqb")]
        bq2s = [sb("bq2_%d" % i, [128, 2]) for i in range(2)]; b_bq2s = [Buf(), Buf()]
        wiqs = [sb("wiq%d" % i, [128, 24]) for i in range(2)]; b_wiqs = [Buf(), Buf()]
        thrbs = [sb("thrb%d" % i, [128, 128]) for i in range(2)]; b_thrbs = [Buf(), Buf()]
        thrBs = [sb("thrB%d" % i, [128, 128]) for i in range(2)]; b_thrBs = [Buf(), Buf()]
        mT = [sb("mT%d" % i, [128, 4, 128], BF16) for i in range(2)]; b_mT = [Buf(), Buf()]
        ptb = [sb("ptb%d" % i, [128, 8, 128], BF16) for i in range(2)]; b_ptb = [Buf(), Buf()]
        kic = [sb("kic%d" % i, [128, 512], BF16) for i in range(2)]; b_kic = [Buf(), Buf()]
        ktc = [sb("ktc%d" % i, [128, 2, 512], BF16) for i in range(2)]; b_ktc = [Buf(), Buf()]
        v1c = [sb("v1c%d" % i, [128, 4, 264], BF16) for i in range(2)]; b_v1c = [Buf(), Buf()]
        accS = mbuf[0:65, :, :].rearrange("p a (h q) -> p (a h) q", q=128); b_accS = b_m
        zrow = sb("zrow", [128, 264], BF16); b_zrow = Buf("zrow")
        cur["st"] = es
        fw.op("sp", lambda e: e.nop(), reads=[b_wb[n] for n in order], writes=[], inc=True)
        fm_c0 = [c * 128 for c in range(8)] + [1024, 1152] + [1536 + j * 128 for j in range(4)]
        for ci_, c0_ in enumerate(fm_c0):
            fw.dma("sp", wfm[ci_, :, :].rearrange("p (a b) -> p a b", b=128), wb["w_in"][:, c0_:c0_ + 128].rearrange("(kc p) n -> p kc n", p=128), writes=[b_wfm])
        for dup in range(2):
            fw.dma("sp", wfm[14, :, :].rearrange("p (a b) -> p a b", b=128)[:, :, dup * 64:(dup + 1) * 64], wb["w_in"][:, 2048:2112].rearrange("(kc p) n -> p kc n", p=128), writes=[b_wfm])
        fw.op("dve", lambda e: e.memset(zrow[:], 0.0), reads=[], writes=[b_zrow])
        fw.op("dve", lambda e: e.memset(v1stg[:], 1.0), reads=[], writes=[b_v1stg[0], b_v1stg[1]])
        set_rr([4, 5, 6, 7])
        for g in range(4):
            fw.dma("sp", tmpf[0][:, 0:128], W["gm_ws"][g, :, :], writes=[b_tmpf[0]])
            bnk = nb()
            fw.op("pe", lambda e, bnk=bnk: e.transpose(out=banks[bnk][:, 0:128], in_=tmpf[0][:, 0:128], identity=ident_f[:]), reads=[b_tmpf[0], b_const], writes=[b_bank[bnk]])
            fw.op("dve", lambda e, bnk=bnk: e.tensor_tensor(out=tmpf[1][:, 0:128], in0=banks[bnk][:, 0:128], in1=triu[:], op=ALU.mult), reads=[b_bank[bnk], b_const], writes=[b_tmpf[1]])
            fw.op("dve", lambda e, g=g: e.tensor_copy(out=wsT[:, g, :], in_=tmpf[1][:, 0:128]), reads=[b_tmpf[1]], writes=[b_const])
            b_wsd = Buf("wsd")
            fw.dma("sp", wsd[g, :, :], tmpf[1][:, 0:128], reads=[b_tmpf[1]], writes=[b_wsd])
            if do_sample:
                fw.op("dve", lambda e: e.memset(tmpf[0][0:64, 0:64], 0.0), reads=[], writes=[b_tmpf[0]])
                for s in range(NSTREAM):
                    fw.dma("sp", tmpf[0][16 * s:16 * s + 16, 16 * s:16 * s + 16], wsd[g, 0:16, 0:16], reads=[b_wsd], writes=[b_tmpf[0]])
                fw.op("dve", lambda e, g=g: e.tensor_copy(out=wsX[:, g, :], in_=tmpf[0][0:64, 0:64]), reads=[b_tmpf[0]], writes=[b_const])
        fw.dma("sp", bsT[:, :], W["gm_bs"].rearrange("g i -> i g"), writes=[b_const], slow=True)
        if do_sample:
            for s in range(NSTREAM):
                fw.dma("sp", bsX[16 * s:16 * s + 16, :], W["gm_bs"][:, 0:16].rearrange("g i -> i g"), writes=[b_const], slow=True)
        fw.dma("sp", tmpf[0][0:32, 0:16], relt[:, :], writes=[b_tmpf[0]])
        fw.dma("sp", tmpf[0][0:32, 16:400], C["c_onehot"][:, :], writes=[b_tmpf[0]])
        fw.dma("sp", stat[0:16, 0:1], relt[15:16, :].rearrange("a h -> h a"), writes=[b_stat], slow=True)
        bnk = nb()
        fw.op("pe", lambda e, bnk=bnk: e.matmul(out=banks[bnk][0:16, 0:384], lhsT=tmpf[0][0:32, 0:16], rhs=tmpf[0][0:32, 16:400], start=True, stop=True),
              reads=[b_tmpf[0]], writes=[b_bank[bnk]])
        fw.op("dve", lambda e, bnk=bnk: e.tensor_scalar(out=tmpf[1][0:16, 0:384], in0=banks[bnk][0:16, 0:384], scalar1=stat[0:16, 0:1], scalar2=None, op0=ALU.subtract),
              reads=[b_bank[bnk], b_stat], writes=[b_tmpf[1]])
        b_gtab = Buf("gtab")
        fw.dma("sp", gtab[:, :], tmpf[1][0:16, 0:384], reads=[b_tmpf[1]], writes=[b_gtab])
        for typ, off in ((0, 0), (1, 128)):
            for hidx in range(16):
                h = hidx_head(hidx)
                ti = hidx % 2
                hank = bass.AP(tensor=gtab.tensor, offset=h * 384 + off, ap=[[1, 128], [1, 128]])
                fw.dma("sp", tmpf[ti][:, 0:128], hank, reads=[b_gtab], writes=[b_tmpf[ti]])
                bnk = nb()
                fw.op("pe", lambda e, bnk=bnk, ti=ti: e.matmul(out=banks[bnk][:, 0:128], lhsT=tmpf[ti][:, 0:128], rhs=exch[:], start=True, stop=True),
                      reads=[b_tmpf[ti], b_const], writes=[b_bank[bnk]])
                fw.op("act", lambda e, bnk=bnk, typ=typ, hidx=hidx: e.copy(out=biasT[:, typ, hidx, :], in_=banks[bnk][:, 0:128]), reads=[b_bank[bnk]], writes=[b_const])

        b_ktS = [Buf("ktS%d" % t) for t in range(NTILE)]
        b_kiS = [Buf("kiS%d" % t) for t in range(NTILE)]
        b_v1S = [Buf("v1S%d" % t) for t in range(NTILE)]

        def p2_load(ti_, t, blocks):
            nonlocal xres, b_xres
            xres = xres_sets[ti_ % 2]; b_xres = b_xres_sets[ti_ % 2]
            for bidx, (ntok, col0) in enumerate(blocks):
                a, b = tile_rows(t, bidx, ntok)
                fw.dma("sp", xres[0:ntok, bidx, :], hS[a:b, :], reads=[b_hS[t]], writes=[b_xres[bidx]])
            to_actT(blocks)

        def p2_store(t, blocks):
            for bidx, (ntok, col0) in enumerate(blocks):
                a, b = tile_rows(t, bidx, ntok)
                fw.dma("sp", h2S[a:b, :], xres[0:ntok, bidx, :], reads=[b_xres[bidx]], writes=[b_h2S[t]])

        for t in range(NTILE):
            r0 = t * TT
            blocks = [(128, 0), (128, 128)]
            p2_load(t, t, blocks)

            def out_k(t=t, r0=r0):
                fw.dma("sp", ktS[:, :, r0:r0 + TT], ktile[:, :, :], reads=[b_ktile], writes=[b_ktS[t]])

            def out_ki(t=t, r0=r0):
                fw.dma("sp", kiS[:, r0:r0 + TT], kitile[:, :], reads=[b_kitile], writes=[b_kiS[t]])
            win_feature_major(blocks, TT, out_k, out_ki)

            def out_kv(bidx, ntok, t=t, r0=r0):
                rr0 = r0 + bidx * 128
                fw.dma("sp", kp[rr0:rr0 + 128, :], stg[:, bidx, 0:256], reads=[b_stg[bidx]], writes=[])
                fw.dma("sp", vp[rr0:rr0 + 128, :], stg[:, bidx, 256:512], reads=[b_stg[bidx]], writes=[])
                fw.dma("sp", v1S[rr0:rr0 + 128, :], v1stg[:, bidx, :, :].rearrange("p a b -> p (a b)"), reads=[b_v1stg[bidx]], writes=[b_v1S[t]])

            def out_kis(bidx, ntok, t=t, r0=r0):
                rr0 = r0 + bidx * 128
                fw.dma("sp", kip[rr0:rr0 + 128, :], stg[:, bidx, 512:576], reads=[b_stg[bidx]], writes=[])
            win_token_major(blocks, out_kv, out_kis)

            gens = []
            for qb2 in range(2):
                qb = 2 * t + qb2
                nkb = qb + 1

                def key_src(kind, k0, n):
                    t0, t1 = k0 // TT, (k0 + n - 1) // TT
                    if kind == "ki":
                        return kiS[:, k0:k0 + n], [b_kiS[i] for i in range(t0, t1 + 1)]
                    if kind == "kt":
                        return ktS[:, :, k0:k0 + n], [b_ktS[i] for i in range(t0, t1 + 1)]
                    return v1S[k0:k0 + n, :].rearrange("(a p) c -> p a c", p=128), [b_v1S[i] for i in range(t0, t1 + 1)]

                def bias_type(kb, qb=qb):
                    return 1 if kb == qb else (0 if kb == qb - 1 else None)
                gens.append(attention(qb2, 128, qb2 * 128, qb2 * 128, key_src, nkb, [(64, qb * 128 + 64, qb * 128 + 128)], bias_type, TOPK))
            run_interleaved(gens)

            gmlp_spatial(blocks, False)
            merge_and_out(blocks)
            mixed_resid_ln(blocks, W["ln2_g"][0:1, :], W["ln2_b"][0:1, :])
            p2_store(t, blocks)

        if do_sample:
            blocks = [(64, 0)]
            b_ktX = [Buf() for _ in range(NSTREAM)]; b_kiX = [Buf() for _ in range(NSTREAM)]; b_v1X = [Buf() for _ in range(NSTREAM)]
            set_rr([4, 5, 6, 7])
            for s in range(NSTREAM):
                for half in range(2):
                    kb0 = half * 4
                    kf = mbuf[:, :, :].rearrange("p a b -> p (a b)")
                    fw.dma("sp", kf[:, 0:1024].rearrange("p (a b) -> p a b", b=256), ck[s, kb0 * 128:(kb0 + 4) * 128, :].rearrange("(a p) c -> p a c", p=128), writes=[b_m])
                    tb = lnA[:].bitcast(BF16)
                    fw.op("dve", lambda e, kf=kf, tb=tb: e.tensor_copy(out=tb[:, 0:1024], in_=kf[:, 0:1024]), reads=[b_m], writes=[b_lnA])
                    for p in range(2):
                        transpose_to(lambda a, p=p, tb=tb: tb[:, a * 256 + p * 128:a * 256 + p * 128 + 128], 4, 128,
                                     lambda c0, n, p=p: ktc[0][:, p, c0 * 128:(c0 + n) * 128].rearrange("q (a b) -> q a b", b=128), [b_lnA], b_ktc[0])
                    fw.dma("sp", ktX[s, :, :, kb0 * 128:(kb0 + 4) * 128], ktc[0][:, :, :], reads=[b_ktc[0]], writes=[b_ktX[s]])
                    fw.dma("sp", kf[:, 0:1024].rearrange("p (a b) -> p a b", b=256), cv[s, kb0 * 128:(kb0 + 4) * 128, :].rearrange("(a p) c -> p a c", p=128), writes=[b_m])
                    vv = v1c[0][:, :, :].rearrange("p a (k d) -> p a k d", d=66)
                    fw.op("dve", lambda e, vv=vv: e.memset(v1c[0][:], 1.0), reads=[], writes=[b_v1c[0]])
                    for a in range(4):
                        fw.op("dve", lambda e, a=a, kf=kf, vv=vv: e.tensor_copy(out=vv[:, a, :, 0:64], in_=kf[:, a * 256:(a + 1) * 256].rearrange("p (k d) -> p k d", d=64)),
                              reads=[b_m], writes=[b_v1c[0]])
                    fw.dma("sp", v1X[s, kb0 * 128:(kb0 + 4) * 128, :].rearrange("(a p) c -> p a c", p=128), v1c[0][:, :, :], reads=[b_v1c[0]], writes=[b_v1X[s]])
                    fw.dma("sp", kf[:, 0:256].rearrange("p (a b) -> p a b", b=64), cki[s, kb0 * 128:(kb0 + 4) * 128, :].rearrange("(a p) c -> p a c", p=128), writes=[b_m])
                    for dup in range(2):
                        fw.op("dve", lambda e, dup=dup, kf=kf, tb=tb: e.tensor_copy(out=tb[:, 0:512].rearrange("p (a b) -> p a b", b=128)[:, :, dup * 64:(dup + 1) * 64],
                                                                                    in_=kf[:, 0:256].rearrange("p (a b) -> p a b", b=64)), reads=[b_m], writes=[b_lnA])
                    transpose_to(lambda a, tb=tb: tb[:, a * 128:(a + 1) * 128], 4, 128,
                                 lambda c0, n: kic[0][:, c0 * 128:(c0 + n) * 128].rearrange("q (a b) -> q a b", b=128), [b_lnA], b_kic[0])
                    fw.dma("sp", kiX[s, :, kb0 * 128:(kb0 + 4) * 128], kic[0][:, :], reads=[b_kic[0]], writes=[b_kiX[s]])
                fw.dma("sp", ktX[s, :, 0, 1024:LX], zrow[:, 0:128], reads=[b_zrow], writes=[b_ktX[s]])
                fw.dma("sp", ktX[s, :, 1, 1024:LX], zrow[:, 0:128], reads=[b_zrow], writes=[b_ktX[s]])
                fw.dma("sp", kiX[s, :, 1024:LX], zrow[:, 0:128], reads=[b_zrow], writes=[b_kiX[s]])
                fw.dma("sp", v1X[s, 1024:LX, :], zrow[:, :], reads=[b_zrow], writes=[b_v1X[s]])

            p2_load(NTILE, NTILE, blocks)

            def out_k_s():
                for s in range(NSTREAM):
                    fw.dma("sp", ktX[s, :, :, 1024:1024 + NS], ktile[:, :, s * NS:(s + 1) * NS], reads=[b_ktile], writes=[b_ktX[s]])

            def out_ki_s():
                for s in range(NSTREAM):
                    fw.dma("sp", kiX[s, :, 1024:1024 + NS], kitile[:, s * NS:(s + 1) * NS], reads=[b_kitile], writes=[b_kiX[s]])
            win_feature_major(blocks, 64, out_k_s, out_ki_s)

            def out_kv_s(bidx, ntok):
                fw.dma("sp", ks[:, :], stg[0:64, 0, 0:256], reads=[b_stg[0]], writes=[])
                fw.dma("sp", vs[:, :], stg[0:64, 0, 256:512], reads=[b_stg[0]], writes=[])
                for s in range(NSTREAM):
                    fw.dma("sp", v1X[s, 1024:1024 + NS, :], v1stg[s * NS:(s + 1) * NS, 0, :, :].rearrange("p a b -> p (a b)"), reads=[b_v1stg[0]], writes=[b_v1X[s]])

            def out_kis_s(bidx, ntok):
                fw.dma("sp", kis[:, :], stg[0:64, 0, 512:576], reads=[b_stg[0]], writes=[])
            win_token_major(blocks, out_kv_s, out_kis_s, gv_out=gvs)
            gens = []
            for s in range(NSTREAM):
                def key_src(kind, k0, n, s=s):
                    if kind == "ki":
                        return kiX[s, :, k0:k0 + n], [b_kiX[s]]
                    if kind == "kt":
                        return ktX[s, :, :, k0:k0 + n], [b_ktX[s]]
                    return v1X[s, k0:k0 + n, :].rearrange("(a p) c -> p a c", p=128), [b_v1X[s]]

                def bias_type(kb):
                    return 1 if kb == 8 else (0 if kb == 7 else None)
                gens.append(attention(s % 2, NS, s * NS, s * NS, key_src, 9, [(NS, PAST + NS, LX)], bias_type, 256.0))
                if s % 2 == 1:
                    run_interleaved(gens)
                    gens = []
            gmlp_spatial(blocks, True)
            merge_and_out(blocks)
            mixed_resid_ln(blocks, W["ln2_g"][0:1, :], W["ln2_b"][0:1, :])
            p2_store(NTILE, blocks)
        fw.barrier()
        fw.emit()
        p2.close()

        def p3_in(t, bidx, ntok):
            a, b = tile_rows(t, bidx, ntok)
            return h2S[a:b, :]

        def p3_out(t, bidx, ntok):
            if t < NTILE:
                a, b = tile_rows(t, bidx, ntok)
                return yp[a:b, :]
            return ys[0:ntok, :]
        ffn_phase(3, p3_in, lambda t: [b_h2S[t]], p3_out, lambda t: [], W["ln3_g"][0:1, :], W["ln3_b"][0:1, :], ["ffn2_wg", "ffn2_wu", "ffn2_wd"])

        fw.finish()
        fw.emit()
        print("n_instr", fw.n, "sems", len(fw.sems))
    return nc


_NC_CACHE = {}


def make_in_maps(inputs, S, n_cores):
    consts = host_consts()
    maps = []
    for c in range(n_cores):
        m = {}
        m["xp"] = np.ascontiguousarray(inputs["x_prompt"][c, :S])
        m["xs"] = np.ascontiguousarray(inputs["x_sample"][4 * c:4 * c + 4].reshape(64, D))
        m["ck"] = np.ascontiguousarray(inputs["cache_k"][0, 4 * c:4 * c + 4].reshape(4, PAST, 256))
        m["cv"] = np.ascontiguousarray(inputs["cache_v"][0, 4 * c:4 * c + 4].reshape(4, PAST, 256))
        m["cki"] = np.ascontiguousarray(inputs["cache_kidx"][0, 4 * c:4 * c + 4])
        m["rel_table"] = np.ascontiguousarray(inputs["rel_table"])
        for n in WNAMES:
            m[n] = np.ascontiguousarray(np.asarray(inputs[n])[0]).reshape(WSHAPES[n])
        m.update(consts)
        maps.append({k: np.asarray(v, dtype=np.float32) for k, v in m.items()})
    return maps


def kernel(**inputs):
    inputs = {k: np.asarray(v) for k, v in inputs.items()}
    B, S = inputs["x_prompt"].shape[:2]
    n_cores = 8
    if S not in _NC_CACHE:
        _NC_CACHE[S] = build_nc(S)
    nc = _NC_CACHE[S]
    maps = make_in_maps(inputs, S, n_cores)
    res = run_bass_kernel_spmd(nc, maps, core_ids=list(range(n_cores)))
    r = res.results
    yp = np.stack([r[c]["yp"] for c in range(8)])
    ys = np.concatenate([r[c]["ys"].reshape(4, NS, D) for c in range(8)])
    kp = np.stack([r[c]["kp"].reshape(S, 4, 64) for c in range(8)])[None]
    vp = np.stack([r[c]["vp"].reshape(S, 4, 64) for c in range(8)])[None]
    kip = np.stack([r[c]["kip"] for c in range(8)])[None]
    ks = np.concatenate([r[c]["ks"].reshape(4, NS, 4, 64) for c in range(8)])[None]
    vs = np.concatenate([r[c]["vs"].reshape(4, NS, 4, 64) for c in range(8)])[None]
    kis = np.concatenate([r[c]["kis"].reshape(4, NS, 64) for c in range(8)])[None]
    gvs = np.concatenate([r[c]["gvs"].reshape(4, NS, D) for c in range(8)])[None]
    f = lambda a: np.ascontiguousarray(a, dtype=np.float32)
    return (f(yp), f(ys), f(kp), f(vp), f(kip), f(ks), f(vs), f(kis), f(gvs))
```

```python
from contextlib import ExitStack
import os
import numpy as np
import concourse.bass as bass
DBG = os.environ.get('KDBG', '')
KATT = int(os.environ.get('KATT', '99'))
import concourse.mybir as mybir
from concourse.bass_utils import run_bass_kernel_spmd

F32 = mybir.dt.float32
BF16 = mybir.dt.bfloat16
U8 = mybir.dt.uint8
AF = mybir.ActivationFunctionType
ALU = mybir.AluOpType
AX = mybir.AxisListType

D = 1024
DFF = 2816
INW = 6216
NEG = -1.0e30
ALPHA = 2.0 ** 0.25
NIT = 24
TT = 256
PAST = 1024
NS = 16
NSTREAM = 4
LX = 1152


class Buf:
    __slots__ = ("name", "w", "r", "excl")

    def __init__(self, name="", excl=False):
        self.name = name
        self.w = None
        self.r = {}
        self.excl = excl


class FW:
    def __init__(self, nc, es):
        self.nc = nc
        self.es = es
        self.eng = {"pe": nc.tensor, "act": nc.scalar, "dve": nc.vector, "pool": nc.gpsimd, "sp": nc.sync}
        self.prog = {e: [] for e in self.eng}
        self.sems = {}
        self.cnt = {}
        self.waited = {e: {} for e in self.eng}
        self.dq = {}
        self.KDMA = 8
        self.n = 0

    def _sem(self, key):
        if key not in self.sems:
            self.sems[key] = self.es.enter_context(self.nc.semaphore("s_" + "_".join(str(k) for k in key)))
            self.cnt[key] = 0
        return self.sems[key]

    def _need(self, e, waits, ev, same_ok):
        if ev is None:
            return
        key, val = ev
        if key == (e,) and same_ok and e == "pe":
            return
        if self.waited[e].get(key, 0) >= val:
            return
        waits[key] = max(waits.get(key, 0), val)

    def _deps(self, e, reads, writes):
        waits = {}
        for b in reads:
            self._need(e, waits, b.w, False)
            if b.excl:
                for k, v in b.r.items():
                    if k != (e,):
                        self._need(e, waits, (k, v), True)
        for b in writes:
            self._need(e, waits, b.w, True)
            for k, v in b.r.items():
                self._need(e, waits, (k, v), True)
        for k, v in waits.items():
            self.waited[e][k] = v
        return [(self._sem(k), v) for k, v in waits.items()]

    def op(self, e, fn, reads=(), writes=(), inc=True):
        self.n += 1
        waits = self._deps(e, reads, writes)
        key = (e,)
        sem = self._sem(key)
        if inc:
            self.cnt[key] += 1
            val = self.cnt[key]
        else:
            val = self.cnt[key] + 1
        for b in reads:
            if b.r.get(key, 0) < val:
                b.r[key] = val
        for b in writes:
            b.w = (key, val)
            b.r = {}
        self.prog[e].append((waits, fn, (sem, 1) if inc else None))

    def dma(self, q, out, in_, reads=(), writes=(), slow=False):
        self.n += 1
        i = self.dq.get(q, 0)
        self.dq[q] = i + 1
        key = ("dma", q, i % self.KDMA)
        sem = self._sem(key)
        waits = self._deps(q, reads, writes)
        prev = self.cnt[key]
        if prev > 0 and self.waited[q].get(key, 0) < prev:
            waits.append((sem, prev))
            self.waited[q][key] = prev
        self.cnt[key] += 16
        val = self.cnt[key]
        for b in reads:
            b.r[key] = val
        for b in writes:
            b.w = (key, val)
            b.r = {}
        if slow:
            self.prog[q].append((waits, lambda eng: eng.dma_start(out=out, in_=in_, allow_slow_non_contiguous=True), (sem, 16)))
        else:
            self.prog[q].append((waits, lambda eng: eng.dma_start(out=out, in_=in_), (sem, 16)))

    def barrier(self):
        for e in self.eng:
            waits = []
            for k, c in self.cnt.items():
                if c > 0 and k != (e,) and self.waited[e].get(k, 0) < c:
                    waits.append((self._sem(k), c))
                    self.waited[e][k] = c
            if waits:
                self.prog[e].append((waits, None, None))

    def finish(self):
        waits = [(self._sem(k), c) for k, c in self.cnt.items() if c > 0]
        self.prog["sp"].append((waits, None, None))

    def emit(self):
        progs = self.prog
        self.prog = {e: [] for e in self.eng}
        for e, lst in progs.items():
            eng = self.eng[e]
            for waits, fn, inc in lst:
                for sem, v in waits:
                    eng.wait_ge(sem, v)
                if fn is not None:
                    ins = fn(eng)
                    if inc is not None:
                        ins.then_inc(inc[0], inc[1])


def hidx_head(hidx):
    return hidx


def chunk_heads(c):
    p, i = divmod(c, 4)
    return 8 * p + i, 8 * p + 4 + i


def rel_bucket_jx(rel):
    import math
    import jax
    import jax.numpy as jnp
    with jax.default_device(jax.devices("cpu")[0]):
        rel = jnp.asarray(rel, jnp.int32)
        half, max_exact = 16, 8
        ret = jnp.where(rel > 0, half, 0)
        n = jnp.abs(rel)
        nf = jnp.maximum(n, 1).astype(jnp.float32)
        large = max_exact + (jnp.log(nf / max_exact) / math.log(128 / max_exact) * (half - max_exact)).astype(jnp.int32)
        large = jnp.minimum(large, half - 1)
        return np.asarray(ret + jnp.where(n < max_exact, n, large))


def rel_bucket_np(rel):
    half, max_exact = 16, 8
    ret = np.where(rel > 0, half, 0)
    n = np.abs(rel)
    nf = np.maximum(n, 1).astype(np.float32)
    large = max_exact + (np.log(nf / np.float32(max_exact)) / np.float32(np.log(128.0 / max_exact))
                         * np.float32(half - max_exact)).astype(np.int32)
    large = np.minimum(large, half - 1)
    return ret + np.where(n < max_exact, n, large)


def host_consts():
    c = {}
    c["c_ident"] = np.eye(128, dtype=np.float32)
    c["c_exch"] = np.ascontiguousarray(np.eye(128, dtype=np.float32)[::-1])
    c["c_triu"] = np.triu(np.ones((128, 128), np.float32))
    rel = np.arange(384, dtype=np.int64) - 255
    try:
        b = rel_bucket_jx(rel.astype(np.int32))
    except Exception:
        b = rel_bucket_np(rel.astype(np.int32))
    oh = np.zeros((32, 384), np.float32)
    oh[b, np.arange(384)] = 1.0
    c["c_onehot"] = oh
    k = np.arange(NIT, dtype=np.float32)
    q = (2.0 ** -(k + 2)).astype(np.float32)
    c["c_bis"] = np.ascontiguousarray(np.broadcast_to(np.concatenate([q, 2 * q])[None, :], (128, 2 * NIT))).astype(np.float32)
    sel = np.zeros((65, 64), np.float32)
    sel[64, :] = 1.0
    c["c_sel"] = sel
    return c


WNAMES = ["ln1_g", "ln1_b", "ffn1_wg", "ffn1_wu", "ffn1_wd", "w_in", "gm_ln_g", "gm_ln_b", "gm_ws", "gm_bs",
          "w_br_a", "w_br_b", "w_out", "ln2_g", "ln2_b", "ffn2_wg", "ffn2_wu", "ffn2_wd", "ln3_g", "ln3_b"]
WSHAPES = {"ln1_g": [1, D], "ln1_b": [1, D], "ffn1_wg": [D, DFF], "ffn1_wu": [D, DFF], "ffn1_wd": [DFF, D],
           "w_in": [D, INW], "gm_ln_g": [1, D], "gm_ln_b": [1, D], "gm_ws": [4, 128, 128], "gm_bs": [4, 128],
           "w_br_a": [D, D], "w_br_b": [D, D], "w_out": [D, D], "ln2_g": [1, D], "ln2_b": [1, D],
           "ffn2_wg": [D, DFF], "ffn2_wu": [D, DFF], "ffn2_wd": [DFF, D], "ln3_g": [1, D], "ln3_b": [1, D]}


def build_nc(S, do_sample=True, nit=NIT, stage=99):
    assert S % TT == 0
    NTILE = S // TT
    TOPK = float(min(256, S // 4))
    nc = bass.Bass("TRN2", target_bir_lowering=False)
    es = ExitStack()

    def din(name, shape):
        return nc.dram_tensor(name, shape, F32, kind="ExternalInput").ap()

    def dout(name, shape):
        return nc.dram_tensor(name, shape, F32, kind="ExternalOutput").ap()

    def dscr(name, shape, dt=BF16):
        return nc.dram_tensor(name, shape, dt, kind="Internal").ap()

    xp = din("xp", [S, D])
    xs = din("xs", [64, D])
    ck = din("ck", [NSTREAM, PAST, 256])
    cv = din("cv", [NSTREAM, PAST, 256])
    cki = din("cki", [NSTREAM, PAST, 64])
    relt = din("rel_table", [32, 16])
    W = {n: din(n, WSHAPES[n]) for n in WNAMES}
    C = {n: din(n, list(v.shape)) for n, v in host_consts().items()}

    yp = dout("yp", [S, D]); ys = dout("ys", [64, D])
    kp = dout("kp", [S, 256]); vp = dout("vp", [S, 256]); kip = dout("kip", [S, 64])
    ks = dout("ks", [64, 256]); vs = dout("vs", [64, 256]); kis = dout("kis", [64, 64])
    gvs = dout("gvs", [64, D])

    wb = {}
    for n in ["ffn1_wg", "ffn1_wu", "ffn1_wd", "w_in", "w_br_a", "w_br_b", "w_out", "ffn2_wg", "ffn2_wu", "ffn2_wd"]:
        wb[n] = dscr(n + "_b", WSHAPES[n])
    b_wb = {n: Buf(n) for n in wb}
    ktS = dscr("ktS", [128, 2, S]); kiS = dscr("kiS", [128, S]); v1S = dscr("v1S", [S, 264])
    ktX = dscr("ktX", [NSTREAM, 128, 2, LX]); kiX = dscr("kiX", [NSTREAM, 128, LX]); v1X = dscr("v1X", [NSTREAM, LX, 264])
    gtab = dscr("gtab", [16, 384], F32)
    wfm = dscr("wfm", [15, 128, 1024])
    b_wfm = Buf("wfm")
    wsd = dscr("wsd", [4, 128, 128], F32)

    with es:
        fw = FW(nc, es)

        cur = {"st": es}

        def sb(name, shape, dt=F32):
            return cur["st"].enter_context(nc.sbuf_tensor(name, shape, dt))

        banks = [es.enter_context(nc.psum_tensor("bank%d" % i, [128, 512], F32)) for i in range(8)]
        banks_bf = [b[:].bitcast(BF16) for b in banks]
        b_bank = [Buf("bank%d" % i, excl=True) for i in range(8)]
        rr = {"list": [4, 5, 6, 7], "i": 0}

        def set_rr(lst):
            rr["list"] = lst
            rr["i"] = 0

        def nb():
            i = rr["list"][rr["i"] % len(rr["list"])]
            rr["i"] += 1
            return i

        ident_f = sb("ident_f", [128, 128]); ident_b = sb("ident_b", [128, 128], BF16)
        exch = sb("exch", [128, 128]); triu = sb("triu", [128, 128])
        bis = sb("bis", [128, 2 * NIT]); selr = sb("selr", [65, 64])
        b_const = Buf("const")
        xres_sets = [sb("xresA", [128, 2, D]), sb("xresB", [128, 2, D])]
        b_xres_sets = [[Buf("xa0"), Buf("xa1")], [Buf("xb0"), Buf("xb1")]]
        xres = xres_sets[0]; b_xres = b_xres_sets[0]
        actT = sb("actT", [128, 8, TT], BF16); b_actT = Buf("actT")
        lnpool = sb("lnpool", [128, 4 * D])
        lnA = lnpool[:, 0:D]; b_lnA = Buf("lnA")
        lnB = lnpool[:, D:2 * D]; b_lnB = Buf("lnB")
        gam = lnpool[:, 2 * D:3 * D]; b_gam = Buf("gam")
        bet = lnpool[:, 3 * D:4 * D]; b_bet = Buf("bet")
        junks = [lnpool[:, 0:2 * D].bitcast(U8), lnpool[:, 2 * D:4 * D].bitcast(U8)]
        b_junks = [[b_lnA, b_lnB], [b_gam, b_bet]]
        stat = sb("stat", [128, 16]); b_stat = Buf("stat")
        tmpb = [sb("tmpb%d" % i, [128, 512], BF16) for i in range(2)]; b_tmpb = [Buf(), Buf()]
        tmpf = [sb("tmpf%d" % i, [128, 512]) for i in range(2)]; b_tmpf = [Buf(), Buf()]
        bq = sb("bq", [128, 2 * NIT + 8]); b_bq = Buf("bq")
        epsc = sb("epsc", [128, 2])
        LMAX = 8192 if S > 1152 else max(S, LX)
        hS = dscr("hS", [S + 64, D], F32); h2S = dscr("h2S", [S + 64, D], F32)
        b_hS = [Buf() for _ in range(NTILE + 1)]; b_h2S = [Buf() for _ in range(NTILE + 1)]

        def wload(src_ap, nparts, shape_free, reads=()):
            i = wctr["i"] % NSLOT
            wctr["i"] += 1
            n = int(np.prod(shape_free))
            dst = wslots[i][0:nparts, 0:n]
            if len(shape_free) == 2:
                dst = dst.rearrange("p (a b) -> p a b", b=shape_free[1])
            fw.dma("sp", dst, src_ap, reads=list(reads), writes=[b_wslot[i]])
            return dst, b_wslot[i]

        def transpose_to(src_ap_fn, nchunks, ntok, dst_fn, src_bufs, dst_buf):
            c = 0
            while c < nchunks:
                n = min(8, nchunks - c)
                bi = nb()
                pv = banks_bf[bi].rearrange("p (a b) -> p a b", b=128)
                for j in range(n):
                    fw.op("pe", lambda e, j=j, c=c, pv=pv: e.transpose(out=pv[:, j, 0:ntok], in_=src_ap_fn(c + j), identity=ident_b[0:ntok, 0:ntok]),
                          reads=src_bufs + [b_const], writes=[b_bank[bi]], inc=(j == n - 1))
                fw.op("act", lambda e, c=c, n=n, pv=pv: e.copy(out=dst_fn(c, n), in_=pv[:, 0:n, 0:ntok]),
                      reads=[b_bank[bi]], writes=[dst_buf])
                c += n

        def layer_norm(src_ap, ntok, g_ap, b_ap, out_ap, src_bufs, out_bufs, tmp=None, b_tmp=None, epsj=0):
            fw.dma("sp", gam[:], g_ap.to_broadcast([128, D]), writes=[b_gam])
            fw.dma("sp", bet[:], b_ap.to_broadcast([128, D]), writes=[b_bet])
            for h in range(2):
                fw.op("dve", lambda e, h=h: e.bn_stats(out=stat[0:ntok, 6 * h:6 * h + 6], in_=src_ap[:, 512 * h:512 * h + 512]),
                      reads=src_bufs, writes=[b_stat])
            fw.op("dve", lambda e: e.bn_aggr(out=stat[0:ntok, 12:14], in_=stat[0:ntok, 0:12]), reads=[b_stat], writes=[b_stat])
            fw.op("act", lambda e: e.activation(out=stat[0:ntok, 14:15], in_=stat[0:ntok, 13:14], func=AF.Sqrt, bias=epsc[0:ntok, epsj:epsj + 1], scale=1.0),
                  reads=[b_stat, b_const], writes=[b_stat])
            fw.op("dve", lambda e: e.reciprocal(out=stat[0:ntok, 14:15], in_=stat[0:ntok, 14:15]), reads=[b_stat], writes=[b_stat])
            fw.op("dve", lambda e: e.tensor_scalar(out=stat[0:ntok, 15:16], in0=stat[0:ntok, 12:13], scalar1=stat[0:ntok, 14:15], scalar2=-1.0,
                                                   op0=ALU.mult, op1=ALU.mult), reads=[b_stat], writes=[b_stat])
            t = tmp if tmp is not None else lnB
            bt = b_tmp if b_tmp is not None else b_lnB
            fw.op("act", lambda e: e.activation(out=t[0:ntok, :], in_=src_ap, func=AF.Identity, scale=stat[0:ntok, 14:15], bias=stat[0:ntok, 15:16]),
                  reads=src_bufs + [b_stat], writes=[bt])
            fw.op("pool", lambda e: e.tensor_tensor(out=t[0:ntok, :], in0=t[0:ntok, :], in1=gam[0:ntok, :], op=ALU.mult),
                  reads=[bt, b_gam], writes=[bt])
            fw.op("pool", lambda e: e.tensor_tensor(out=out_ap, in0=t[0:ntok, :], in1=bet[0:ntok, :], op=ALU.add),
                  reads=[bt, b_bet], writes=out_bufs)

        def cast_T(src_ap, ntok, blk, src_bufs, dst, dst_buf, col0, nch=8):
            tb = lnA[:].bitcast(BF16)
            fw.op("dve", lambda e: e.tensor_copy(out=tb[0:ntok, 0:nch * 128], in_=src_ap), reads=src_bufs, writes=[b_lnA])
            transpose_to(lambda c: tb[0:ntok, c * 128:(c + 1) * 128], nch, ntok,
                         lambda c0, n: dst[:, c0:c0 + n, col0:col0 + ntok], [b_lnA], dst_buf)

        def ffn(blocks):
            set_rr([4, 5, 6, 7])
            nchunks = [(n0, min(512, DFF - n0)) for n0 in range(0, DFF, 512)]
            for (n0, ncol) in nchunks:
                for bidx, (ntok, col0) in enumerate(blocks):
                    gb, ub = nb(), nb()
                    for wres, bnk in ((wgR, gb), (wuR, ub)):
                        for kc in range(8):
                            fw.op("pe", lambda e, wres=wres, kc=kc, bnk=bnk, ntok=ntok, col0=col0, ncol=ncol, n0=n0:
                                  e.matmul(out=banks[bnk][0:ntok, 0:ncol], lhsT=actT[:, kc, col0:col0 + ntok], rhs=wres[:, kc, n0:n0 + ncol],
                                           start=(kc == 0), stop=(kc == 7)),
                                  reads=[b_actT, b_wres], writes=[b_bank[bnk]], inc=(kc == 7))
                    ti = (bidx + n0 // 512) % 2
                    fw.op("act", lambda e, ti=ti, gb=gb, ntok=ntok, ncol=ncol: e.activation(out=tmpb[ti][0:ntok, 0:ncol], in_=banks[gb][0:ntok, 0:ncol], func=AF.Silu),
                          reads=[b_bank[gb]], writes=[b_tmpb[ti]])
                    hb = b_hidblk[bidx]
                    fw.op("dve", lambda e, ti=ti, ub=ub, ntok=ntok, ncol=ncol, n0=n0, bidx=bidx:
                          e.tensor_tensor(out=hidblk[bidx][0:ntok, n0:n0 + ncol], in0=banks[ub][0:ntok, 0:ncol], in1=tmpb[ti][0:ntok, 0:ncol], op=ALU.mult),
                          reads=[b_bank[ub], b_tmpb[ti]], writes=[hb])
            for bidx, (ntok, col0) in enumerate(blocks):
                transpose_to(lambda c, bidx=bidx, ntok=ntok: hidblk[bidx][0:ntok, c * 128:(c + 1) * 128], 22, ntok,
                             lambda c0, n, col0=col0, ntok=ntok: hidT[:, c0:c0 + n, col0:col0 + ntok], [b_hidblk[bidx]], b_hidT)
            for kc in range(22):
                for bidx, (ntok, col0) in enumerate(blocks):
                    for hf in range(2):
                        bnk = 2 * bidx + hf
                        fw.op("pe", lambda e, kc=kc, bnk=bnk, ntok=ntok, col0=col0, hf=hf:
                              e.matmul(out=banks[bnk][0:ntok, :], lhsT=hidT[:, kc, col0:col0 + ntok], rhs=wdR[:, kc, hf * 512:(hf + 1) * 512],
                                       start=(kc == 0), stop=(kc == 21)),
                              reads=[b_hidT, b_wres], writes=[b_bank[bnk]], inc=(kc == 21))

        def ffn_phase(which, in_ap_fn, in_bufs_fn, out_ap_fn, out_bufs_fn, g_ap, b_ap, wnames):
            nonlocal wgR, wuR, wdR, hidT, hidblk, b_hidblk, b_hidT, b_wres, xres, b_xres
            ph = ExitStack()
            cur["st"] = ph
            wgR = sb("wgR%d" % which, [128, 8, DFF], BF16); wuR = sb("wuR%d" % which, [128, 8, DFF], BF16)
            wdR = sb("wdR%d" % which, [128, 22, D], BF16)
            hidT = sb("hidT%d" % which, [128, 22, TT], BF16); b_hidT = Buf("hidT")
            hidblk = [sb("hid%d_%d" % (which, i), [128, DFF], BF16) for i in range(2)]
            b_hidblk = [Buf("hida"), Buf("hidb")]
            cur["st"] = es
            b_wres = Buf("wres")
            for kc in range(8):
                fw.dma("sp", wgR[:, kc, :], wb[wnames[0]][kc * 128:(kc + 1) * 128, :], writes=[b_wres])
                fw.dma("sp", wuR[:, kc, :], wb[wnames[1]][kc * 128:(kc + 1) * 128, :], writes=[b_wres])
            for kc in range(0, 22, 2):
                fw.dma("sp", wdR[:, kc:kc + 2, :], wb[wnames[2]][kc * 128:(kc + 2) * 128, :].rearrange("(a p) n -> p a n", p=128), writes=[b_wres])
            tiles = [(t, [(128, 0), (128, 128)]) for t in range(NTILE)]
            if do_sample:
                tiles.append((NTILE, [(64, 0)]))
            def front(ti_):
                nonlocal xres, b_xres
                t, blocks = tiles[ti_]
                xres = xres_sets[ti_ % 2]; b_xres = b_xres_sets[ti_ % 2]
                for bidx, (ntok, col0) in enumerate(blocks):
                    fw.dma("sp", xres[0:ntok, bidx, :], in_ap_fn(t, bidx, ntok), reads=in_bufs_fn(t), writes=[b_xres[bidx]])
                to_actT(blocks)
            front(0)
            for ti_, (t, blocks) in enumerate(tiles):
                xres = xres_sets[ti_ % 2]; b_xres = b_xres_sets[ti_ % 2]
                ffn(blocks)
                if ti_ + 1 < len(tiles):
                    front(ti_ + 1)
                    xres = xres_sets[ti_ % 2]; b_xres = b_xres_sets[ti_ % 2]
                resid_ln(blocks, g_ap, b_ap, final_out=lambda bidx, ntok, t=t: out_ap_fn(t, bidx, ntok), out_bufs=out_bufs_fn(t))
            fw.barrier()
            fw.emit()
            ph.close()

        wgR = wuR = wdR = hidT = hidblk = b_hidblk = b_hidT = b_wres = None

        def resid_ln(blocks, g_ap, b_ap, final_out=None, out_bufs=()):
            for bidx, (ntok, col0) in enumerate(blocks):
                for hf in range(2):
                    bnk = 2 * bidx + hf
                    fw.op("dve", lambda e, bnk=bnk, hf=hf, ntok=ntok, bidx=bidx, xr=xres: e.scalar_tensor_tensor(
                        out=lnA[0:ntok, hf * 512:(hf + 1) * 512], in0=banks[bnk][0:ntok, :], scalar=float(0.5 / ALPHA),
                        in1=xr[0:ntok, bidx, hf * 512:(hf + 1) * 512], op0=ALU.mult, op1=ALU.add),
                        reads=[b_bank[bnk], b_xres[bidx]], writes=[b_lnA])
                layer_norm(lnA[0:ntok, :], ntok, g_ap, b_ap, xres[0:ntok, bidx, :], [b_lnA], [b_xres[bidx]], epsj=1)
                if final_out is not None:
                    fw.dma("sp", final_out(bidx, ntok), xres[0:ntok, bidx, :], reads=[b_xres[bidx]], writes=list(out_bufs))

        def mixed_resid_ln(blocks, g_ap, b_ap):
            for bidx, (ntok, col0) in enumerate(blocks):
                for hf in range(2):
                    bnk = 2 * bidx + hf
                    fw.op("dve", lambda e, bnk=bnk, hf=hf, ntok=ntok, bidx=bidx, xr=xres: e.scalar_tensor_tensor(
                        out=lnA[0:ntok, hf * 512:(hf + 1) * 512], in0=xr[0:ntok, bidx, hf * 512:(hf + 1) * 512], scalar=ALPHA,
                        in1=banks[bnk][0:ntok, :], op0=ALU.mult, op1=ALU.add),
                        reads=[b_bank[bnk], b_xres[bidx]], writes=[b_lnA])
                layer_norm(lnA[0:ntok, :], ntok, g_ap, b_ap, xres[0:ntok, bidx, :], [b_lnA], [b_xres[bidx]])

        def to_actT(blocks):
            for bidx, (ntok, col0) in enumerate(blocks):
                cast_T(xres[0:ntok, bidx, :], ntok, bidx, [b_xres[bidx]], actT, b_actT, col0)

        winb = wb["w_in"]
        b_win = b_wb["w_in"]

        def win_feature_major(blocks, Ttok, out_k, out_ki):
            set_rr([4, 5, 6, 7])
            specs = [("q", c, c * 128, qT[:, c, 0:Ttok], b_qT) for c in range(8)]
            specs += [("k", p, 1024 + p * 128, ktile[:, p, 0:Ttok], b_ktile) for p in range(2)]
            specs += [("qi", j, 1536 + j * 128, qiT[:, j, 0:Ttok], b_qiT) for j in range(4)]
            specs += [("ki", 0, 2048, kitile[:, 0:Ttok], b_kitile)]
            for si_, (kind, idx, c0, dst, dbuf) in enumerate(specs):
                ap_, b_ = wload(wfm[si_, :, :].rearrange("p (a b) -> p a b", b=128), 128, [8, 128], reads=[b_wfm])
                bnk = nb()
                for kc in range(8):
                    fw.op("pe", lambda e, ap_=ap_, kc=kc, bnk=bnk: e.matmul(out=banks[bnk][:, 0:Ttok], lhsT=ap_[:, kc, :], rhs=actT[:, kc, 0:Ttok],
                                                                             start=(kc == 0), stop=(kc == 7)),
                          reads=[b_actT, b_], writes=[b_bank[bnk]], inc=(kc == 7))
                fw.op("act", lambda e, dst=dst, bnk=bnk: e.copy(out=dst, in_=banks[bnk][:, 0:Ttok]), reads=[b_bank[bnk]], writes=[dbuf])
            out_k()
            out_ki()

        def win_token_major(blocks, out_kv, out_kis, gv_out=None):
            set_rr([4, 5, 6, 7])
            units = [wload(winb[kh * 512:(kh + 1) * 512, 1024:1536].rearrange("(kc p) n -> p kc n", p=128), 128, [4, 512]) for kh in range(2)]
            units2 = wload(winb[:, 2048:2120].rearrange("(kc p) n -> p kc n", p=128), 128, [8, 72])
            for bidx, (ntok, col0) in enumerate(blocks):
                b1, b2 = nb(), nb()
                for kc in range(8):
                    ap_, b_ = units[kc // 4]
                    fw.op("pe", lambda e, ap_=ap_, kc=kc, b1=b1, ntok=ntok, col0=col0: e.matmul(out=banks[b1][0:ntok, :], lhsT=actT[:, kc, col0:col0 + ntok], rhs=ap_[:, kc % 4, :],
                                                                                                 start=(kc == 0), stop=(kc == 7)),
                          reads=[b_actT, b_], writes=[b_bank[b1]], inc=(kc == 7))
                for kc in range(8):
                    ap_, b_ = units2
                    fw.op("pe", lambda e, ap_=ap_, kc=kc, b2=b2, ntok=ntok, col0=col0: e.matmul(out=banks[b2][0:ntok, 0:72], lhsT=actT[:, kc, col0:col0 + ntok], rhs=ap_[:, kc, :],
                                                                                                 start=(kc == 0), stop=(kc == 7)),
                          reads=[b_actT, b_], writes=[b_bank[b2]], inc=(kc == 7))
                fw.op("act", lambda e, bidx=bidx, b1=b1, ntok=ntok: e.copy(out=stg[0:ntok, bidx, 0:512], in_=banks[b1][0:ntok, :]), reads=[b_bank[b1]], writes=[b_stg[bidx]])
                fw.op("act", lambda e, bidx=bidx, b2=b2, ntok=ntok: e.copy(out=stg[0:ntok, bidx, 512:584], in_=banks[b2][0:ntok, 0:72]), reads=[b_bank[b2]], writes=[b_stg[bidx]])
                if 'A' not in DBG:
                    fw.op("dve", lambda e, bidx=bidx, b1=b1, ntok=ntok: e.tensor_copy(out=v1stg[0:ntok, bidx, :, 0:64], in_=stg[0:ntok, bidx, 256:512].rearrange("p (a b) -> p a b", b=64)),
                          reads=[b_stg[bidx], b_const], writes=[b_v1stg[bidx]])
                if 'B' not in DBG:
                    out_kv(bidx, ntok)
                if 'C' not in DBG:
                    out_kis(bidx, ntok)
            if stage < 3.4:
                return
            for ch in range(4):
                c0 = 2120 + ch * 512
                units = [wload(winb[kh * 512:(kh + 1) * 512, c0:c0 + 512].rearrange("(kc p) n -> p kc n", p=128), 128, [4, 512]) for kh in range(2)]
                for bidx, (ntok, col0) in enumerate(blocks):
                    b1 = nb()
                    for kc in range(8):
                        ap_, b_ = units[kc // 4]
                        fw.op("pe", lambda e, ap_=ap_, kc=kc, b1=b1, ntok=ntok, col0=col0: e.matmul(out=banks[b1][0:ntok, :], lhsT=actT[:, kc, col0:col0 + ntok], rhs=ap_[:, kc % 4, :],
                                                                                                     start=(kc == 0), stop=(kc == 7)),
                              reads=[b_actT, b_], writes=[b_bank[b1]], inc=(kc == 7))
                    if ch < 2:
                        fw.op("act", lambda e, b1=b1, ntok=ntok, bidx=bidx, ch=ch: e.activation(out=ubuf[0:ntok, bidx, ch * 512:(ch + 1) * 512], in_=banks[b1][0:ntok, :], func=AF.Gelu_apprx_tanh),
                              reads=[b_bank[b1]], writes=[b_u[bidx]])
                    else:
                        fw.op("act", lambda e, b1=b1, ntok=ntok, bidx=bidx, ch=ch: e.activation(out=mbuf[0:ntok, bidx, (ch - 2) * 512:(ch - 1) * 512], in_=banks[b1][0:ntok, :], func=AF.Gelu_apprx_tanh),
                              reads=[b_bank[b1]], writes=[b_m])
            if stage < 3.6:
                return
            for bidx, (ntok, col0) in enumerate(blocks):
                layer_norm(mbuf[0:ntok, bidx, :], ntok, W["gm_ln_g"][0:1, :], W["gm_ln_b"][0:1, :], lnA[0:ntok, :], [b_m], [b_lnA])
                fw.op("dve", lambda e, ntok=ntok, bidx=bidx: e.tensor_copy(out=vgb[0:ntok, bidx, :], in_=lnA[0:ntok, :]), reads=[b_lnA], writes=[b_vg[bidx]])
                if gv_out is not None:
                    fw.dma("sp", gv_out[0:ntok, :], lnA[0:ntok, :], reads=[b_lnA], writes=[])

        def gstep(g):
            try:
                return next(g)
            except StopIteration:
                return "END"

        def run_to(g, tag):
            while True:
                r = gstep(g)
                if r == tag or r == "END":
                    return

        def interleave(ga, taga, na, gb, tagb, nb_):
            da = db = 0
            fa = fb = False
            while not (fa and fb):
                pick_a = (not fa) and (fb or da * nb_ <= db * na)
                if pick_a:
                    r = gstep(ga); da += 1
                    if r == taga or r == "END":
                        fa = True
                else:
                    r = gstep(gb); db += 1
                    if r == tagb or r == "END":
                        fb = True

        def run_pair(g0, g1, L0, L1, nkb0):
            run_to(g0, "A")
            interleave(g0, "B", nit * 4, g1, "A", ((L1 + 511) // 512) * 8 + 8)
            interleave(g1, "B", nit * 4, g0, "END", 2 * nkb0 + 8)
            run_to(g1, "END")

        def attention(par, nq, qc0, hT_cols, key_src, nkb, diag_fill, bias_type, topk):
            L = nkb * 128
            Ssb = S2[par]; b_S = b_S2[par]
            bq = bqs[par]; b_bq = b_bqs[par]
            bq2 = bq2s[par]; b_bq2 = b_bq2s[par]
            wiq = wiqs[par]; b_wiq = b_wiqs[par]
            thrb = thrbs[par]; b_thrb = b_thrbs[par]
            thrB = thrBs[par]; b_thrB = b_thrBs[par]
            jk = junks[par]; b_jk = b_junks[par]
            u2 = wload(winb[:, 2112:2120].rearrange("(kc p) n -> p kc n", p=128), 128, [8, 8])
            bnk = nb()
            for kc in range(8):
                fw.op("pe", lambda e, kc=kc, bnk=bnk: e.matmul(out=banks[bnk][0:nq, 0:8], lhsT=actT[:, kc, hT_cols:hT_cols + nq], rhs=u2[0][:, kc, :],
                                                                start=(kc == 0), stop=(kc == 7)),
                      reads=[b_actT, u2[1]], writes=[b_bank[bnk]], inc=(kc == 7))
            fw.op("act", lambda e, bnk=bnk: e.activation(out=wiq[0:nq, 0:8], in_=banks[bnk][0:nq, 0:8], func=AF.Abs, scale=float(64 ** -0.5 * 8 ** -0.5)),
                  reads=[b_bank[bnk]], writes=[b_wiq])
            fw.op("dve", lambda e, bnk=bnk: e.tensor_scalar(out=wiq[0:nq, 8:16], in0=banks[bnk][0:nq, 0:8], scalar1=0.0, scalar2=2.0,
                                                            op0=ALU.is_ge, op1=ALU.mult), reads=[b_bank[bnk]], writes=[b_wiq])
            fw.op("dve", lambda e: e.tensor_scalar(out=wiq[0:nq, 8:16], in0=wiq[0:nq, 8:16], scalar1=-1.0, scalar2=None, op0=ALU.add), reads=[b_wiq], writes=[b_wiq])
            if KATT < 1:
                return
            for k0 in range(0, L, 512):
                ncol = min(512, L - k0)
                ci = (k0 // 512) % 2
                src, sbufs = key_src("ki", k0, ncol)
                fw.dma("sp", kic[ci][:, 0:ncol], src, reads=sbufs, writes=[b_kic[ci]])
                for h in range(8):
                    bnk = nb()
                    hp = (h % 2) * 64
                    fw.op("pe", lambda e, h=h, hp=hp, bnk=bnk, ci=ci, ncol=ncol: e.matmul(out=banks[bnk][0:nq, 0:ncol], lhsT=qiT[hp:hp + 64, h // 2, qc0:qc0 + nq],
                                                                                           rhs=kic[ci][hp:hp + 64, 0:ncol], start=True, stop=True),
                          reads=[b_qiT, b_kic[ci]], writes=[b_bank[bnk]])
                    ti = h % 2
                    fw.op("act", lambda e, h=h, bnk=bnk, ti=ti, ncol=ncol: e.activation(out=tmpf[ti][0:nq, 0:ncol], in_=banks[bnk][0:nq, 0:ncol], func=AF.Relu,
                                                                                         scale=wiq[0:nq, h:h + 1]),
                          reads=[b_bank[bnk], b_wiq], writes=[b_tmpf[ti]])
                    if h == 0:
                        fw.op("dve", lambda e, ti=ti, k0=k0, ncol=ncol: e.tensor_scalar(out=Ssb[0:nq, k0:k0 + ncol], in0=tmpf[ti][0:nq, 0:ncol], scalar1=wiq[0:nq, 8:9], scalar2=None, op0=ALU.mult),
                              reads=[b_tmpf[ti], b_wiq], writes=[b_S])
                    else:
                        fw.op("dve", lambda e, h=h, ti=ti, k0=k0, ncol=ncol: e.scalar_tensor_tensor(out=Ssb[0:nq, k0:k0 + ncol], in0=tmpf[ti][0:nq, 0:ncol], scalar=wiq[0:nq, 8 + h:9 + h],
                                                                                                    in1=Ssb[0:nq, k0:k0 + ncol], op0=ALU.mult, op1=ALU.add),
                              reads=[b_tmpf[ti], b_wiq, b_S], writes=[b_S])
                    yield
            if KATT < 2:
                return
            NQ = 2 * NIT
            fw.op("dve", lambda e: e.tensor_reduce(out=bq[0:nq, NQ:NQ + 1], in_=Ssb[0:nq, 0:L], axis=AX.X, op=ALU.max), reads=[b_S], writes=[b_bq])
            fw.op("dve", lambda e: e.tensor_reduce(out=bq[0:nq, NQ + 1:NQ + 2], in_=Ssb[0:nq, 0:L], axis=AX.X, op=ALU.min), reads=[b_S], writes=[b_bq])
            if diag_fill is not None:
                for (r1, c0_, c1_) in diag_fill:
                    fw.op("dve", lambda e, r1=r1, c0_=c0_, c1_=c1_: e.memset(Ssb[0:r1, c0_:c1_], NEG), reads=[b_bq], writes=[b_S])
            fw.op("dve", lambda e: e.tensor_tensor(out=bq[0:nq, NQ + 2:NQ + 3], in0=bq[0:nq, NQ:NQ + 1], in1=bq[0:nq, NQ + 1:NQ + 2], op=ALU.subtract), reads=[b_bq], writes=[b_bq])
            fw.op("dve", lambda e: e.scalar_tensor_tensor(out=bq[0:nq, NQ + 1:NQ + 2], in0=bq[0:nq, NQ + 2:NQ + 3], scalar=-0.01, in1=bq[0:nq, NQ + 1:NQ + 2], op0=ALU.mult, op1=ALU.add),
                  reads=[b_bq], writes=[b_bq])
            fw.op("dve", lambda e: e.tensor_scalar(out=bq[0:nq, NQ + 1:NQ + 2], in0=bq[0:nq, NQ + 1:NQ + 2], scalar1=-1e-6, scalar2=None, op0=ALU.add), reads=[b_bq], writes=[b_bq])
            fw.op("dve", lambda e: e.tensor_scalar(out=bq[0:nq, NQ + 2:NQ + 3], in0=bq[0:nq, NQ + 2:NQ + 3], scalar1=1.02, scalar2=2e-6, op0=ALU.mult, op1=ALU.add), reads=[b_bq], writes=[b_bq])
            fw.op("dve", lambda e: e.tensor_scalar(out=bq[0:nq, 0:NQ], in0=bis[0:nq, :], scalar1=bq[0:nq, NQ + 2:NQ + 3], scalar2=None, op0=ALU.mult), reads=[b_bq, b_const], writes=[b_bq])
            fw.op("dve", lambda e: e.scalar_tensor_tensor(out=bq[0:nq, NQ + 3:NQ + 4], in0=bq[0:nq, NQ + 2:NQ + 3], scalar=0.5, in1=bq[0:nq, NQ + 1:NQ + 2], op0=ALU.mult, op1=ALU.add),
                  reads=[b_bq], writes=[b_bq])
            mid = bq[0:nq, NQ + 3:NQ + 4]
            cnt = bq[0:nq, NQ + 4:NQ + 5]
            t2 = bq[0:nq, NQ + 5:NQ + 6]
            Ld = L if L < 1024 else int(round(0.46 * L / 64.0)) * 64
            nA = L - Ld
            yield "A"
            b_jkD = Buf("jkD"); b_jkA = Buf("jkA"); b_cnt = Buf("cnt")
            cntv = bq2[0:nq, 1:2]
            for k in range(nit):
                edge = list(b_jk) if (k == 0 or k == nit - 1) else []
                if nA > 0:
                    fw.op("act", lambda e: e.activation(out=jk[0:nq, Ld:L], in_=Ssb[0:nq, Ld:L], func=AF.Sign, bias=mid, scale=-1.0, accum_out=bq2[0:nq, 0:1]),
                          reads=[b_S, b_bq], writes=[b_jkA, b_bq2] + edge)
                fw.op("dve", lambda e: e.tensor_scalar(out=jk[0:nq, 0:Ld], in0=Ssb[0:nq, 0:Ld], scalar1=mid, scalar2=0.0, op0=ALU.is_ge, op1=ALU.add, accum_out=cntv),
                      reads=[b_S, b_bq], writes=[b_jkD, b_cnt] + edge)
                yield
                if nA > 0:
                    fw.op("dve", lambda e: e.scalar_tensor_tensor(out=cntv, in0=bq2[0:nq, 0:1], scalar=-0.5, in1=cntv, op0=ALU.mult, op1=ALU.add), reads=[b_cnt, b_bq2], writes=[b_cnt])
                    yield
                fw.op("dve", lambda e, k=k: e.tensor_scalar(out=t2, in0=cntv, scalar1=float(topk - 0.5 * nA), scalar2=bq[0:nq, NIT + k:NIT + k + 1], op0=ALU.is_ge, op1=ALU.mult), reads=[b_cnt, b_bq], writes=[b_cnt])
                yield
                fw.op("dve", lambda e, k=k: e.tensor_scalar(out=mid, in0=mid, scalar1=t2, scalar2=bq[0:nq, k:k + 1], op0=ALU.add, op1=ALU.subtract), reads=[b_cnt, b_bq], writes=[b_bq])
                yield
            yield "B"
            set_rr([4, 5, 6, 7])
            if KATT < 3:
                return
            fw.op("dve", lambda e: e.tensor_tensor(out=bq[0:nq, NQ + 6:NQ + 7], in0=mid, in1=bq[0:nq, nit - 1:nit], op=ALU.subtract), reads=[b_bq], writes=[b_bq])
            fw.op("dve", lambda e: e.tensor_copy(out=thrb[0:nq, :], in_=bq[0:nq, NQ + 6:NQ + 7].to_broadcast([nq, 128])), reads=[b_bq], writes=[b_thrb])
            bnk = nb()
            fw.op("pe", lambda e, bnk=bnk: e.transpose(out=banks[bnk][:, 0:nq], in_=thrb[0:nq, :], identity=ident_f[0:nq, 0:nq]), reads=[b_thrb, b_const], writes=[b_bank[bnk]])
            fw.op("act", lambda e, bnk=bnk: e.copy(out=thrB[:, 0:nq], in_=banks[bnk][:, 0:nq]), reads=[b_bank[bnk]], writes=[b_thrB])
            if KATT < 4:
                return
            for b4 in range(4):
                fw.op("dve", lambda e, b4=b4: e.memset(banks[b4][0:65, :], 0.0), reads=[], writes=[b_bank[b4]])
            ptc = {"i": 0, "g": 0}
            PAIRS = [(4, 5), (6, 7)]

            def prep(kb):
                g4 = kb // 4
                if kb % 4 == 0:
                    n4 = min(4, nkb - kb)
                    ci = g4 % 2
                    src, sbufs = key_src("kt", kb * 128, n4 * 128)
                    fw.dma("sp", ktc[ci][:, :, 0:n4 * 128], src, reads=sbufs, writes=[b_ktc[ci]])
                    src, sbufs = key_src("v1", kb * 128, n4 * 128)
                    fw.dma("sp", v1c[ci][:, 0:n4, :], src, reads=sbufs, writes=[b_v1c[ci]])
                    bnk = PAIRS[ptc["g"] % 2][0]
                    pv = banks[bnk][:].rearrange("p (a b) -> p a b", b=128)
                    for j in range(n4):
                        fw.op("pe", lambda e, j=j, kb=kb, pv=pv: e.transpose(out=pv[:, j, 0:nq], in_=Ssb[0:nq, (kb + j) * 128:(kb + j + 1) * 128], identity=ident_f[0:nq, 0:nq]),
                              reads=[b_S, b_const], writes=[b_bank[bnk]], inc=(j == n4 - 1))
                    fw.op("dve", lambda e, ci=ci, pv=pv, n4=n4: e.tensor_tensor(out=mT[ci][:, 0:n4, 0:nq], in0=pv[:, 0:n4, 0:nq], in1=thrB[:, 0:nq].unsqueeze(1).to_broadcast([128, n4, nq]), op=ALU.is_ge),
                          reads=[b_bank[bnk], b_thrB], writes=[b_mT[ci]])

            def emit_qk(kb, G):
                ci = (kb // 4) % 2
                kj = kb % 4
                bAB = PAIRS[ptc["g"] % 2]
                ptc["g"] += 1
                lgs = [banks[b_][:].rearrange("p (a b) -> p a b", b=128) for b_ in bAB]
                if nq == 128:
                    for u in range(2):
                        fw.op("pe", lambda e, u=u, G=G, ci=ci, kj=kj, lgs=lgs: e.matmul(
                            out=lgs[u][:, 0:4, 0:nq], lhsT=ktc[ci][u * 64:u * 64 + 64, G, kj * 128:(kj + 1) * 128],
                            rhs=qT[u * 64:u * 64 + 64, 4 * G:4 * G + 4, qc0:qc0 + nq], start=True, stop=True),
                            reads=[b_ktc[ci], b_qT], writes=[b_bank[bAB[u]]])
                else:
                    for i in range(4):
                        c = 4 * G + i
                        for u in range(2):
                            fw.op("pe", lambda e, i=i, c=c, u=u, G=G, ci=ci, kj=kj, lgs=lgs: e.matmul(
                                out=lgs[u][:, i, 0:nq], lhsT=ktc[ci][u * 64:u * 64 + 64, G, kj * 128:(kj + 1) * 128],
                                rhs=qT[u * 64:u * 64 + 64, c, qc0:qc0 + nq], start=True, stop=True),
                                reads=[b_ktc[ci], b_qT], writes=[b_bank[bAB[u]]], inc=(i == 3))
                return bAB, lgs

            def emit_rest(kb, G, bAB, lgs):
                ci = (kb // 4) % 2
                kj = kb % 4
                bt = bias_type(kb)
                pti = ptc["i"]
                for u in range(2):
                    hb = 2 * G + u
                    bnk = bAB[u]
                    lg = lgs[u]
                    P = pt[pti % 4]
                    bP = b_pt[pti % 4]
                    pti += 1
                    if 'E' in DBG:
                        continue
                    if bt is None:
                        fw.op("act", lambda e, lg=lg, P=P: e.activation(out=P[:, :, 0:nq], in_=lg[:, :, 0:nq], func=AF.Exp, scale=0.125), reads=[b_bank[bnk]], writes=[bP])
                    else:
                        tf = tmpf[hb % 2][:].rearrange("p (a b) -> p a b", b=128)
                        fw.op("dve", lambda e, lg=lg, tf=tf, bt=bt, hb=hb: e.scalar_tensor_tensor(out=tf[:, :, 0:nq], in0=lg[:, :, 0:nq], scalar=0.125, in1=biasT[:, bt, 4 * hb:4 * hb + 4, 0:nq],
                                                                                                  op0=ALU.mult, op1=ALU.add),
                              reads=[b_bank[bnk], b_const], writes=[b_tmpf[hb % 2]])
                        fw.op("act", lambda e, tf=tf, P=P: e.activation(out=P[:, :, 0:nq], in_=tf[:, :, 0:nq], func=AF.Exp), reads=[b_tmpf[hb % 2]], writes=[bP])
                    fw.op("dve", lambda e, P=P, ci=ci, kj=kj: e.tensor_tensor(out=P[:, :, 0:nq], in0=P[:, :, 0:nq], in1=mT[ci][:, kj:kj + 1, 0:nq].to_broadcast([128, 4, nq]), op=ALU.mult),
                          reads=[bP, b_mT[ci]], writes=[bP])
                    accv = banks[hb][:].rearrange("p (a b) -> p a b", b=128)
                    if nq == 128:
                        fw.op("pe", lambda e, hb=hb, P=P, ci=ci, kj=kj, accv=accv: e.matmul(out=accv[0:65, 0:4, 0:nq], lhsT=v1c[ci][:, kj, hb * 66:hb * 66 + 65], rhs=P[:, 0:4, 0:nq],
                                                                                            start=False, stop=False, skip_group_check=True),
                              reads=[b_v1c[ci], bP], writes=[b_bank[hb]])
                    else:
                        for j in range(4):
                            fw.op("pe", lambda e, j=j, hb=hb, P=P, ci=ci, kj=kj, accv=accv: e.matmul(out=accv[0:65, j, 0:nq], lhsT=v1c[ci][:, kj, hb * 66:hb * 66 + 65], rhs=P[:, j, 0:nq],
                                                                                                     start=False, stop=False, skip_group_check=True),
                                  reads=[b_v1c[ci], bP], writes=[b_bank[hb]], inc=(j == 3))
                ptc["i"] = pti

            pend = None
            for kb in range(nkb):
                for G in range(2):
                    if G == 0:
                        prep(kb)
                    cur_ = emit_qk(kb, G)
                    if pend is not None:
                        emit_rest(*pend)
                    pend = (kb, G) + cur_
                    yield
            if pend is not None:
                emit_rest(*pend)
            if KATT < 6:
                return
            for hg in range(4):
                accv = banks[hg][:].rearrange("p (a b) -> p a b", b=128)
                fw.op("act", lambda e, hg=hg, accv=accv: e.copy(out=accS[:, 4 * hg:4 * hg + 4, 0:nq], in_=accv[0:65, :, 0:nq]), reads=[b_bank[hg]], writes=[b_accS])
            fw.op("dve", lambda e: e.reciprocal(out=accS[64:65, :, 0:nq], in_=accS[64:65, :, 0:nq]), reads=[b_accS, b_const], writes=[b_accS])
            for hg in range(4):
                bnk = nb()
                bc = banks[bnk][:].rearrange("p (a b) -> p a b", b=128)
                for j in range(4):
                    fw.op("pe", lambda e, hg=hg, j=j, bc=bc: e.matmul(out=bc[0:64, j, 0:nq], lhsT=selr[64:65, :], rhs=accS[64:65, 4 * hg + j, 0:nq], start=True, stop=True),
                          reads=[b_accS, b_const], writes=[b_bank[bnk]], inc=(j == 3))
                fw.op("dve", lambda e, hg=hg, bc=bc: e.tensor_tensor(out=aT[:, 4 * hg:4 * hg + 4, qc0:qc0 + nq], in0=accS[0:64, 4 * hg:4 * hg + 4, 0:nq], in1=bc[0:64, :, 0:nq], op=ALU.mult),
                      reads=[b_accS, b_bank[bnk]], writes=[b_aT])

        def gmlp_spatial(blocks, sample):
            set_rr([4, 5, 6, 7])
            for bidx, (ntok, col0) in enumerate(blocks):
                b1, b2 = nb(), nb()
                for g in range(4):
                    bnk = b1 if g < 2 else b2
                    lw = wsX[0:ntok, g, 0:ntok] if sample else wsT[:, g, :]
                    fw.op("pe", lambda e, g=g, bnk=bnk, lw=lw, ntok=ntok, bidx=bidx: e.matmul(out=banks[bnk][0:ntok, (g % 2) * 256:(g % 2) * 256 + 256], lhsT=lw,
                                                                                              rhs=vgb[0:ntok, bidx, g * 256:(g + 1) * 256], start=True, stop=True),
                          reads=[b_vg[bidx], b_const], writes=[b_bank[bnk]], inc=(g % 2 == 1))
                tb = lnA[:].bitcast(BF16)
                for g in range(4):
                    bnk = b1 if g < 2 else b2
                    bsc = bsX[0:ntok, g:g + 1] if sample else bsT[0:ntok, g:g + 1]
                    fw.op("dve", lambda e, g=g, bnk=bnk, bsc=bsc, ntok=ntok, bidx=bidx: e.scalar_tensor_tensor(
                        out=tb[0:ntok, g * 256:(g + 1) * 256], in0=banks[bnk][0:ntok, (g % 2) * 256:(g % 2) * 256 + 256], scalar=bsc,
                        in1=ubuf[0:ntok, bidx, g * 256:(g + 1) * 256], op0=ALU.add, op1=ALU.mult),
                        reads=[b_bank[bnk], b_u[bidx], b_const], writes=[b_lnA])
                transpose_to(lambda c, ntok=ntok: tb[0:ntok, c * 128:(c + 1) * 128], 8, ntok,
                             lambda c0, n, col0=col0, ntok=ntok: gT[:, c0:c0 + n, col0:col0 + ntok], [b_lnA], b_gT)

        def proj_tok(blocks, lhs_fn, nk, w_fn, lhs_bufs):
            for k in range(nk):
                ap_, b_ = w_fn(k)
                for bidx, (ntok, col0) in enumerate(blocks):
                    for hf in range(2):
                        bnk = 2 * bidx + hf
                        fw.op("pe", lambda e, k=k, ap_=ap_, bnk=bnk, ntok=ntok, col0=col0, hf=hf: e.matmul(out=banks[bnk][0:ntok, :], lhsT=lhs_fn(k, ntok, col0), rhs=ap_[:, hf * 512:(hf + 1) * 512],
                                                                                                           start=(k == 0), stop=(k == nk - 1)),
                              reads=lhs_bufs + [b_], writes=[b_bank[bnk]], inc=(k == nk - 1 or (bidx == len(blocks) - 1 and hf == 1)))

        def unit_rows(src, r0, nrows, per):
            return wload(src[r0:r0 + per * nrows, :].rearrange("(a p) n -> p a n", p=nrows), nrows, [per, D])

        def merge_and_out(blocks):
            wa = wb["w_br_a"]; wbb = wb["w_br_b"]; wo = wb["w_out"]
            def w_cols(c0):
                cache = {}

                def f(k):
                    if k % 2 == 0:
                        cache["u"] = wload(winb[k * 128:(k + 2) * 128, c0:c0 + D].rearrange("(a p) n -> p a n", p=128), 128, [2, D])
                    ap_, b_ = cache["u"]
                    return ap_[:, k % 2, :], b_
                return f

            def w_rows128(src):
                cache = {}

                def f(k):
                    if k % 2 == 0:
                        cache["u"] = unit_rows(src, k * 128, 128, 2)
                    ap_, b_ = cache["u"]
                    return ap_[:, k % 2, :], b_
                return f

            def w_rows_heads():
                cache = {}

                def f(k):
                    h = hidx_head(k)
                    ap_, b_ = wload(wa[h * 64:(h + 1) * 64, :], 64, [D])
                    return ap_, b_
                return f

            hT_fn = lambda k, ntok, col0: actT[:, k, col0:col0 + ntok]
            set_rr([4, 5, 6, 7])
            proj_tok(blocks, hT_fn, 8, w_cols(4168), [b_actT])
            for bidx, (ntok, col0) in enumerate(blocks):
                for hf in range(2):
                    fw.op("act", lambda e, bidx=bidx, hf=hf, ntok=ntok: e.activation(out=sa[0:ntok, bidx, hf * 512:(hf + 1) * 512], in_=banks[2 * bidx + hf][0:ntok, :], func=AF.Sigmoid),
                          reads=[b_bank[2 * bidx + hf]], writes=[b_sa])
            proj_tok(blocks, lambda k, ntok, col0: aT[:, k, col0:col0 + ntok], 16, w_rows_heads(), [b_aT])
            for bidx, (ntok, col0) in enumerate(blocks):
                for hf in range(2):
                    fw.op("dve", lambda e, bidx=bidx, hf=hf, ntok=ntok: e.tensor_tensor(out=mbuf[0:ntok, bidx, hf * 512:(hf + 1) * 512], in0=banks[2 * bidx + hf][0:ntok, :],
                                                                                         in1=sa[0:ntok, bidx, hf * 512:(hf + 1) * 512], op=ALU.mult),
                          reads=[b_bank[2 * bidx + hf], b_sa], writes=[b_m])
            proj_tok(blocks, hT_fn, 8, w_cols(5192), [b_actT])
            for bidx, (ntok, col0) in enumerate(blocks):
                for hf in range(2):
                    fw.op("act", lambda e, bidx=bidx, hf=hf, ntok=ntok: e.activation(out=sa[0:ntok, bidx, hf * 512:(hf + 1) * 512], in_=banks[2 * bidx + hf][0:ntok, :], func=AF.Sigmoid),
                          reads=[b_bank[2 * bidx + hf]], writes=[b_sa])
            proj_tok(blocks, lambda k, ntok, col0: gT[:, k, col0:col0 + ntok], 8, w_rows128(wbb), [b_gT])
            for bidx, (ntok, col0) in enumerate(blocks):
                for hf in range(2):
                    fw.op("dve", lambda e, bidx=bidx, hf=hf, ntok=ntok: e.tensor_tensor(out=tmpf[hf][0:ntok, :], in0=banks[2 * bidx + hf][0:ntok, :],
                                                                                         in1=sa[0:ntok, bidx, hf * 512:(hf + 1) * 512], op=ALU.mult),
                          reads=[b_bank[2 * bidx + hf], b_sa], writes=[b_tmpf[hf]])
                    fw.op("dve", lambda e, bidx=bidx, hf=hf, ntok=ntok: e.tensor_tensor(out=mbuf[0:ntok, bidx, hf * 512:(hf + 1) * 512], in0=mbuf[0:ntok, bidx, hf * 512:(hf + 1) * 512],
                                                                                         in1=tmpf[hf][0:ntok, :], op=ALU.add),
                          reads=[b_tmpf[hf], b_m], writes=[b_m])
            for bidx, (ntok, col0) in enumerate(blocks):
                cast_T(mbuf[0:ntok, bidx, :], ntok, bidx, [b_m], qT, b_qT, col0)
            proj_tok(blocks, lambda k, ntok, col0: qT[:, k, col0:col0 + ntok], 8, w_rows128(wo), [b_qT])

        for n, t_ in (("c_ident", ident_f), ("c_exch", exch), ("c_triu", triu), ("c_bis", bis), ("c_sel", selr)):
            fw.dma("sp", t_[:], C[n][:, :], writes=[b_const])
        fw.op("dve", lambda e: e.tensor_copy(out=ident_b[:], in_=ident_f[:]), reads=[b_const], writes=[b_const])
        fw.op("dve", lambda e: e.memset(epsc[:, 0:1], 1e-5), reads=[], writes=[b_const])
        fw.op("dve", lambda e: e.memset(epsc[:, 1:2], 1e-5 / (ALPHA * ALPHA)), reads=[b_const], writes=[b_const])
        order = ["ffn1_wg", "ffn1_wu", "ffn1_wd", "w_in", "w_br_a", "w_br_b", "w_out", "ffn2_wg", "ffn2_wu", "ffn2_wd"]
        for n in order:
            src = W[n]
            R = WSHAPES[n][0]
            if n == "w_in":
                for r0 in range(0, R, 256):
                    fw.dma("pool", wb[n][r0:r0 + 256, 1024:INW], src[r0:r0 + 256, 1024:INW], writes=[b_wb[n]])
                for c in range(8):
                    for u, h in enumerate(chunk_heads(c)):
                        fw.dma("pool", wb[n][:, c * 128 + u * 64:c * 128 + u * 64 + 64], src[:, h * 64:(h + 1) * 64], writes=[b_wb[n]])
            else:
                for r0 in range(0, R, 256):
                    fw.dma("pool", wb[n][r0:r0 + 256, :], src[r0:r0 + 256, :], writes=[b_wb[n]])
        fw.op("sp", lambda e: e.nop(), reads=[b_wb[n] for n in order[:3]], writes=[], inc=True)

        def tile_rows(t, bidx, ntok):
            r = (t * TT + bidx * 128) if t < NTILE else S
            return r, r + ntok

        def p1_in(t, bidx, ntok):
            if t < NTILE:
                a, b = tile_rows(t, bidx, ntok)
                return xp[a:b, :]
            return xs[:, :]

        def p1_out(t, bidx, ntok):
            a, b = tile_rows(t, bidx, ntok)
            return hS[a:b, :]
        ffn_phase(1, p1_in, lambda t: [], p1_out, lambda t: [b_hS[t]], W["ln1_g"][0:1, :], W["ln1_b"][0:1, :], ["ffn1_wg", "ffn1_wu", "ffn1_wd"])

        p2 = ExitStack()
        cur["st"] = p2
        wsT = sb("wsT", [128, 4, 128], BF16)
        bsT = sb("bsT", [128, 4])
        wsX = sb("wsX", [64, 4, 64], BF16)
        bsX = sb("bsX", [64, 4])
        biasT = sb("biasT", [128, 2, 16, 128])
        qT = sb("qT", [128, 8, TT], BF16); b_qT = Buf("qT")
        qiT = sb("qiT", [128, 4, TT], BF16); b_qiT = Buf("qiT")
        ktile = sb("ktile", [128, 2, TT], BF16); b_ktile = Buf("ktile")
        kitile = sb("kitile", [128, TT], BF16); b_kitile = Buf("kitile")
        ubuf = sb("ubuf", [128, 2, D], BF16); b_u = [Buf("u0"), Buf("u1")]
        vgb = sb("vgb", [128, 2, D], BF16); b_vg = [Buf("vg0"), Buf("vg1")]
        aT = sb("aT", [64, 16, TT], BF16); b_aT = Buf("aT")
        gT = sb("gT", [128, 8, TT], BF16); b_gT = Buf("gT")
        sa = sb("sa", [128, 2, D], BF16); b_sa = Buf("sa")
        mbuf = sb("mbuf", [128, 2, D]); b_m = Buf("m")
        stg = sb("stg", [128, 2, 512 + 72]); b_stg = [Buf(), Buf()]
        v1stg = sb("v1stg", [128, 2, 4, 66], BF16); b_v1stg = [Buf(), Buf()]
        NSLOT = 4
        wslots = [sb("wslot%d" % i, [128, 2048], BF16) for i in range(NSLOT)]
        b_wslot = [Buf("ws%d" % i) for i in range(NSLOT)]
        wctr = {"i": 0}
        S2 = [sb("Ssb%d" % i, [128, LMAX]) for i in range(2)]; b_S2 = [Buf("S0"), Buf("S1")]
        bqs = [bq, sb("bqb", [128, 2 * NIT + 8])]; b_bqs = [b_bq, Buf("bqb")]
        bq2s = [sb("bq2_%d" % i, [128, 2]) for i in range(2)]; b_bq2s = [Buf(), Buf()]
        wiqs = [sb("wiq%d" % i, [128, 24]) for i in range(2)]; b_wiqs = [Buf(), Buf()]
        thrbs = [sb("thrb%d" % i, [128, 128]) for i in range(2)]; b_thrbs = [Buf(), Buf()]
        thrBs = [sb("thrB%d" % i, [128, 128]) for i in range(2)]; b_thrBs = [Buf(), Buf()]
        mT = [sb("mT%d" % i, [128, 4, 128], BF16) for i in range(2)]; b_mT = [Buf(), Buf()]
        pt = [sb("pt%d" % i, [128, 4, 128], BF16) for i in range(4)]; b_pt = [Buf(), Buf(), Buf(), Buf()]
        kic = [sb("kic%d" % i, [128, 512], BF16) for i in range(2)]; b_kic = [Buf(), Buf()]
        ktc = [sb("ktc%d" % i, [128, 2, 512], BF16) for i in range(2)]; b_ktc = [Buf(), Buf()]
        v1c = [sb("v1c%d" % i, [128, 4, 264], BF16) for i in range(2)]; b_v1c = [Buf(), Buf()]
        accS = mbuf[0:65, :, :].rearrange("p a (h q) -> p (a h) q", q=128); b_accS = b_m
        zrow = sb("zrow", [128, 264], BF16); b_zrow = Buf("zrow")
        cur["st"] = es
        fw.op("sp", lambda e: e.nop(), reads=[b_wb[n] for n in order], writes=[], inc=True)
        fm_c0 = [c * 128 for c in range(8)] + [1024, 1152] + [1536 + j * 128 for j in range(4)]
        for ci_, c0_ in enumerate(fm_c0):
            fw.dma("sp", wfm[ci_, :, :].rearrange("p (a b) -> p a b", b=128), wb["w_in"][:, c0_:c0_ + 128].rearrange("(kc p) n -> p kc n", p=128), writes=[b_wfm])
        for dup in range(2):
            fw.dma("sp", wfm[14, :, :].rearrange("p (a b) -> p a b", b=128)[:, :, dup * 64:(dup + 1) * 64], wb["w_in"][:, 2048:2112].rearrange("(kc p) n -> p kc n", p=128), writes=[b_wfm])
        fw.op("dve", lambda e: e.memset(zrow[:], 0.0), reads=[], writes=[b_zrow])
        fw.op("dve", lambda e: e.memset(v1stg[:], 1.0), reads=[], writes=[b_v1stg[0], b_v1stg[1]])
        set_rr([4, 5, 6, 7])
        for g in range(4):
            fw.dma("sp", tmpf[0][:, 0:128], W["gm_ws"][g, :, :], writes=[b_tmpf[0]])
            bnk = nb()
            fw.op("pe", lambda e, bnk=bnk: e.transpose(out=banks[bnk][:, 0:128], in_=tmpf[0][:, 0:128], identity=ident_f[:]), reads=[b_tmpf[0], b_const], writes=[b_bank[bnk]])
            fw.op("dve", lambda e, bnk=bnk: e.tensor_tensor(out=tmpf[1][:, 0:128], in0=banks[bnk][:, 0:128], in1=triu[:], op=ALU.mult), reads=[b_bank[bnk], b_const], writes=[b_tmpf[1]])
            fw.op("dve", lambda e, g=g: e.tensor_copy(out=wsT[:, g, :], in_=tmpf[1][:, 0:128]), reads=[b_tmpf[1]], writes=[b_const])
            b_wsd = Buf("wsd")
            fw.dma("sp", wsd[g, :, :], tmpf[1][:, 0:128], reads=[b_tmpf[1]], writes=[b_wsd])
            if do_sample:
                fw.op("dve", lambda e: e.memset(tmpf[0][0:64, 0:64], 0.0), reads=[], writes=[b_tmpf[0]])
                for s in range(NSTREAM):
                    fw.dma("sp", tmpf[0][16 * s:16 * s + 16, 16 * s:16 * s + 16], wsd[g, 0:16, 0:16], reads=[b_wsd], writes=[b_tmpf[0]])
                fw.op("dve", lambda e, g=g: e.tensor_copy(out=wsX[:, g, :], in_=tmpf[0][0:64, 0:64]), reads=[b_tmpf[0]], writes=[b_const])
        fw.dma("sp", bsT[:, :], W["gm_bs"].rearrange("g i -> i g"), writes=[b_const], slow=True)
        if do_sample:
            for s in range(NSTREAM):
                fw.dma("sp", bsX[16 * s:16 * s + 16, :], W["gm_bs"][:, 0:16].rearrange("g i -> i g"), writes=[b_const], slow=True)
        fw.dma("sp", tmpf[0][0:32, 0:16], relt[:, :], writes=[b_tmpf[0]])
        fw.dma("sp", tmpf[0][0:32, 16:400], C["c_onehot"][:, :], writes=[b_tmpf[0]])
        fw.dma("sp", stat[0:16, 0:1], relt[15:16, :].rearrange("a h -> h a"), writes=[b_stat], slow=True)
        bnk = nb()
        fw.op("pe", lambda e, bnk=bnk: e.matmul(out=banks[bnk][0:16, 0:384], lhsT=tmpf[0][0:32, 0:16], rhs=tmpf[0][0:32, 16:400], start=True, stop=True),
              reads=[b_tmpf[0]], writes=[b_bank[bnk]])
        fw.op("dve", lambda e, bnk=bnk: e.tensor_scalar(out=tmpf[1][0:16, 0:384], in0=banks[bnk][0:16, 0:384], scalar1=stat[0:16, 0:1], scalar2=None, op0=ALU.subtract),
              reads=[b_bank[bnk], b_stat], writes=[b_tmpf[1]])
        b_gtab = Buf("gtab")
        fw.dma("sp", gtab[:, :], tmpf[1][0:16, 0:384], reads=[b_tmpf[1]], writes=[b_gtab])
        for typ, off in ((0, 0), (1, 128)):
            for hidx in range(16):
                h = hidx_head(hidx)
                ti = hidx % 2
                hank = bass.AP(tensor=gtab.tensor, offset=h * 384 + off, ap=[[1, 128], [1, 128]])
                fw.dma("sp", tmpf[ti][:, 0:128], hank, reads=[b_gtab], writes=[b_tmpf[ti]])
                bnk = nb()
                fw.op("pe", lambda e, bnk=bnk, ti=ti: e.matmul(out=banks[bnk][:, 0:128], lhsT=tmpf[ti][:, 0:128], rhs=exch[:], start=True, stop=True),
                      reads=[b_tmpf[ti], b_const], writes=[b_bank[bnk]])
                fw.op("act", lambda e, bnk=bnk, typ=typ, hidx=hidx: e.copy(out=biasT[:, typ, hidx, :], in_=banks[bnk][:, 0:128]), reads=[b_bank[bnk]], writes=[b_const])

        b_ktS = [Buf("ktS%d" % t) for t in range(NTILE)]
        b_kiS = [Buf("kiS%d" % t) for t in range(NTILE)]
        b_v1S = [Buf("v1S%d" % t) for t in range(NTILE)]

        def p2_load(ti_, t, blocks):
            nonlocal xres, b_xres
            xres = xres_sets[ti_ % 2]; b_xres = b_xres_sets[ti_ % 2]
            for bidx, (ntok, col0) in enumerate(blocks):
                a, b = tile_rows(t, bidx, ntok)
                fw.dma("sp", xres[0:ntok, bidx, :], hS[a:b, :], reads=[b_hS[t]], writes=[b_xres[bidx]])
            to_actT(blocks)

        def p2_store(t, blocks):
            for bidx, (ntok, col0) in enumerate(blocks):
                a, b = tile_rows(t, bidx, ntok)
                fw.dma("sp", h2S[a:b, :], xres[0:ntok, bidx, :], reads=[b_xres[bidx]], writes=[b_h2S[t]])

        for t in range(NTILE):
            r0 = t * TT
            blocks = [(128, 0), (128, 128)]
            p2_load(t, t, blocks)

            def out_k(t=t, r0=r0):
                fw.dma("sp", ktS[:, :, r0:r0 + TT], ktile[:, :, :], reads=[b_ktile], writes=[b_ktS[t]])

            def out_ki(t=t, r0=r0):
                fw.dma("sp", kiS[:, r0:r0 + TT], kitile[:, :], reads=[b_kitile], writes=[b_kiS[t]])
            win_feature_major(blocks, TT, out_k, out_ki)

            def out_kv(bidx, ntok, t=t, r0=r0):
                rr0 = r0 + bidx * 128
                fw.dma("sp", kp[rr0:rr0 + 128, :], stg[:, bidx, 0:256], reads=[b_stg[bidx]], writes=[])
                fw.dma("sp", vp[rr0:rr0 + 128, :], stg[:, bidx, 256:512], reads=[b_stg[bidx]], writes=[])
                fw.dma("sp", v1S[rr0:rr0 + 128, :], v1stg[:, bidx, :, :].rearrange("p a b -> p (a b)"), reads=[b_v1stg[bidx]], writes=[b_v1S[t]])

            def out_kis(bidx, ntok, t=t, r0=r0):
                rr0 = r0 + bidx * 128
                fw.dma("sp", kip[rr0:rr0 + 128, :], stg[:, bidx, 512:576], reads=[b_stg[bidx]], writes=[])
            win_token_major(blocks, out_kv, out_kis)

            gens = []
            for qb2 in range(2):
                qb = 2 * t + qb2
                nkb = qb + 1

                def key_src(kind, k0, n):
                    t0, t1 = k0 // TT, (k0 + n - 1) // TT
                    if kind == "ki":
                        return kiS[:, k0:k0 + n], [b_kiS[i] for i in range(t0, t1 + 1)]
                    if kind == "kt":
                        return ktS[:, :, k0:k0 + n], [b_ktS[i] for i in range(t0, t1 + 1)]
                    return v1S[k0:k0 + n, :].rearrange("(a p) c -> p a c", p=128), [b_v1S[i] for i in range(t0, t1 + 1)]

                def bias_type(kb, qb=qb):
                    return 1 if kb == qb else (0 if kb == qb - 1 else None)
                gens.append(attention(qb2, 128, qb2 * 128, qb2 * 128, key_src, nkb, [(64, qb * 128 + 64, qb * 128 + 128)], bias_type, TOPK))
            run_pair(gens[0], gens[1], (2 * t + 1) * 128, (2 * t + 2) * 128, 2 * t + 1)

            gmlp_spatial(blocks, False)
            merge_and_out(blocks)
            mixed_resid_ln(blocks, W["ln2_g"][0:1, :], W["ln2_b"][0:1, :])
            p2_store(t, blocks)

        if do_sample:
            blocks = [(64, 0)]
            b_ktX = [Buf() for _ in range(NSTREAM)]; b_kiX = [Buf() for _ in range(NSTREAM)]; b_v1X = [Buf() for _ in range(NSTREAM)]
            set_rr([4, 5, 6, 7])
            for s in range(NSTREAM):
                for half in range(2):
                    kb0 = half * 4
                    kf = mbuf[:, :, :].rearrange("p a b -> p (a b)")
                    fw.dma("sp", kf[:, 0:1024].rearrange("p (a b) -> p a b", b=256), ck[s, kb0 * 128:(kb0 + 4) * 128, :].rearrange("(a p) c -> p a c", p=128), writes=[b_m])
                    tb = lnA[:].bitcast(BF16)
                    fw.op("dve", lambda e, kf=kf, tb=tb: e.tensor_copy(out=tb[:, 0:1024], in_=kf[:, 0:1024]), reads=[b_m], writes=[b_lnA])
                    for p in range(2):
                        transpose_to(lambda a, p=p, tb=tb: tb[:, a * 256 + p * 128:a * 256 + p * 128 + 128], 4, 128,
                                     lambda c0, n, p=p: ktc[0][:, p, c0 * 128:(c0 + n) * 128].rearrange("q (a b) -> q a b", b=128), [b_lnA], b_ktc[0])
                    fw.dma("sp", ktX[s, :, :, kb0 * 128:(kb0 + 4) * 128], ktc[0][:, :, :], reads=[b_ktc[0]], writes=[b_ktX[s]])
                    fw.dma("sp", kf[:, 0:1024].rearrange("p (a b) -> p a b", b=256), cv[s, kb0 * 128:(kb0 + 4) * 128, :].rearrange("(a p) c -> p a c", p=128), writes=[b_m])
                    vv = v1c[0][:, :, :].rearrange("p a (k d) -> p a k d", d=66)
                    fw.op("dve", lambda e, vv=vv: e.memset(v1c[0][:], 1.0), reads=[], writes=[b_v1c[0]])
                    for a in range(4):
                        fw.op("dve", lambda e, a=a, kf=kf, vv=vv: e.tensor_copy(out=vv[:, a, :, 0:64], in_=kf[:, a * 256:(a + 1) * 256].rearrange("p (k d) -> p k d", d=64)),
                              reads=[b_m], writes=[b_v1c[0]])
                    fw.dma("sp", v1X[s, kb0 * 128:(kb0 + 4) * 128, :].rearrange("(a p) c -> p a c", p=128), v1c[0][:, :, :], reads=[b_v1c[0]], writes=[b_v1X[s]])
                    fw.dma("sp", kf[:, 0:256].rearrange("p (a b) -> p a b", b=64), cki[s, kb0 * 128:(kb0 + 4) * 128, :].rearrange("(a p) c -> p a c", p=128), writes=[b_m])
                    for dup in range(2):
                        fw.op("dve", lambda e, dup=dup, kf=kf, tb=tb: e.tensor_copy(out=tb[:, 0:512].rearrange("p (a b) -> p a b", b=128)[:, :, dup * 64:(dup + 1) * 64],
                                                                                    in_=kf[:, 0:256].rearrange("p (a b) -> p a b", b=64)), reads=[b_m], writes=[b_lnA])
                    transpose_to(lambda a, tb=tb: tb[:, a * 128:(a + 1) * 128], 4, 128,
                                 lambda c0, n: kic[0][:, c0 * 128:(c0 + n) * 128].rearrange("q (a b) -> q a b", b=128), [b_lnA], b_kic[0])
                    fw.dma("sp", kiX[s, :, kb0 * 128:(kb0 + 4) * 128], kic[0][:, :], reads=[b_kic[0]], writes=[b_kiX[s]])
                fw.dma("sp", ktX[s, :, 0, 1024:LX], zrow[:, 0:128], reads=[b_zrow], writes=[b_ktX[s]])
                fw.dma("sp", ktX[s, :, 1, 1024:LX], zrow[:, 0:128], reads=[b_zrow], writes=[b_ktX[s]])
                fw.dma("sp", kiX[s, :, 1024:LX], zrow[:, 0:128], reads=[b_zrow], writes=[b_kiX[s]])
                fw.dma("sp", v1X[s, 1024:LX, :], zrow[:, :], reads=[b_zrow], writes=[b_v1X[s]])

            p2_load(NTILE, NTILE, blocks)

            def out_k_s():
                for s in range(NSTREAM):
                    fw.dma("sp", ktX[s, :, :, 1024:1024 + NS], ktile[:, :, s * NS:(s + 1) * NS], reads=[b_ktile], writes=[b_ktX[s]])

            def out_ki_s():
                for s in range(NSTREAM):
                    fw.dma("sp", kiX[s, :, 1024:1024 + NS], kitile[:, s * NS:(s + 1) * NS], reads=[b_kitile], writes=[b_kiX[s]])
            win_feature_major(blocks, 64, out_k_s, out_ki_s)

            def out_kv_s(bidx, ntok):
                fw.dma("sp", ks[:, :], stg[0:64, 0, 0:256], reads=[b_stg[0]], writes=[])
                fw.dma("sp", vs[:, :], stg[0:64, 0, 256:512], reads=[b_stg[0]], writes=[])
                for s in range(NSTREAM):
                    fw.dma("sp", v1X[s, 1024:1024 + NS, :], v1stg[s * NS:(s + 1) * NS, 0, :, :].rearrange("p a b -> p (a b)"), reads=[b_v1stg[0]], writes=[b_v1X[s]])

            def out_kis_s(bidx, ntok):
                fw.dma("sp", kis[:, :], stg[0:64, 0, 512:576], reads=[b_stg[0]], writes=[])
            win_token_major(blocks, out_kv_s, out_kis_s, gv_out=gvs)
            gens = []
            for s in range(NSTREAM):
                def key_src(kind, k0, n, s=s):
                    if kind == "ki":
                        return kiX[s, :, k0:k0 + n], [b_kiX[s]]
                    if kind == "kt":
                        return ktX[s, :, :, k0:k0 + n], [b_ktX[s]]
                    return v1X[s, k0:k0 + n, :].rearrange("(a p) c -> p a c", p=128), [b_v1X[s]]

                def bias_type(kb):
                    return 1 if kb == 8 else (0 if kb == 7 else None)
                gens.append(attention(s % 2, NS, s * NS, s * NS, key_src, 9, [(NS, PAST + NS, LX)], bias_type, 256.0))
                if s % 2 == 1:
                    run_pair(gens[0], gens[1], LX, LX, 9)
                    gens = []
            gmlp_spatial(blocks, True)
            merge_and_out(blocks)
            mixed_resid_ln(blocks, W["ln2_g"][0:1, :], W["ln2_b"][0:1, :])
            p2_store(NTILE, blocks)
        fw.barrier()
        fw.emit()
        p2.close()

        def p3_in(t, bidx, ntok):
            a, b = tile_rows(t, bidx, ntok)
            return h2S[a:b, :]

        def p3_out(t, bidx, ntok):
            if t < NTILE:
                a, b = tile_rows(t, bidx, ntok)
                return yp[a:b, :]
            return ys[0:ntok, :]
        ffn_phase(3, p3_in, lambda t: [b_h2S[t]], p3_out, lambda t: [], W["ln3_g"][0:1, :], W["ln3_b"][0:1, :], ["ffn2_wg", "ffn2_wu", "ffn2_wd"])

        fw.finish()
        fw.emit()
        print("n_instr", fw.n, "sems", len(fw.sems))
    return nc


_NC_CACHE = {}


def make_in_maps(inputs, S, n_cores):
    consts = host_consts()
    maps = []
    for c in range(n_cores):
        m = {}
        m["xp"] = np.ascontiguousarray(inputs["x_prompt"][c, :S])
        m["xs"] = np.ascontiguousarray(inputs["x_sample"][4 * c:4 * c + 4].reshape(64, D))
        m["ck"] = np.ascontiguousarray(inputs["cache_k"][0, 4 * c:4 * c + 4].reshape(4, PAST, 256))
        m["cv"] = np.ascontiguousarray(inputs["cache_v"][0, 4 * c:4 * c + 4].reshape(4, PAST, 256))
        m["cki"] = np.ascontiguousarray(inputs["cache_kidx"][0, 4 * c:4 * c + 4])
        m["rel_table"] = np.ascontiguousarray(inputs["rel_table"])
        for n in WNAMES:
            m[n] = np.ascontiguousarray(np.asarray(inputs[n])[0]).reshape(WSHAPES[n])
        m.update(consts)
        maps.append({k: np.asarray(v, dtype=np.float32) for k, v in m.items()})
    return maps


def kernel(**inputs):
    inputs = {k: np.asarray(v) for k, v in inputs.items()}
    B, S = inputs["x_prompt"].shape[:2]
    n_cores = 8
    if S not in _NC_CACHE:
        _NC_CACHE[S] = build_nc(S)
    nc = _NC_CACHE[S]
    maps = make_in_maps(inputs, S, n_cores)
    res = run_bass_kernel_spmd(nc, maps, core_ids=list(range(n_cores)))
    r = res.results
    yp = np.stack([r[c]["yp"] for c in range(8)])
    ys = np.concatenate([r[c]["ys"].reshape(4, NS, D) for c in range(8)])
    kp = np.stack([r[c]["kp"].reshape(S, 4, 64) for c in range(8)])[None]
    vp = np.stack([r[c]["vp"].reshape(S, 4, 64) for c in range(8)])[None]
    kip = np.stack([r[c]["kip"] for c in range(8)])[None]
    ks = np.concatenate([r[c]["ks"].reshape(4, NS, 4, 64) for c in range(8)])[None]
    vs = np.concatenate([r[c]["vs"].reshape(4, NS, 4, 64) for c in range(8)])[None]
    kis = np.concatenate([r[c]["kis"].reshape(4, NS, 64) for c in range(8)])[None]
    gvs = np.concatenate([r[c]["gvs"].reshape(4, NS, D) for c in range(8)])[None]
    f = lambda a: np.ascontiguousarray(a, dtype=np.float32)
    return (f(yp), f(ys), f(kp), f(vp), f(kip), f(ks), f(vs), f(kis), f(gvs))
```
